# Optimizing a Trainium2 kernel written in Bass

```python
import jax, jax.numpy as jnp
from jax import lax
import numpy as np

D_MODEL = 4096
BATCH = 2
SEQ = 4096
DEPTH = 2
DEC_BATCH = 8
DEC_SEQ = 32
PAST_LEN = 4096

CHUNK = 64
N_A_LAYERS = DEPTH // 2
N_B_LAYERS = DEPTH - N_A_LAYERS
SGU_CHUNK = 128
SGU_GROUPS = 8
SGU_WIDTH = D_MODEL
SGU_GROUP_WIDTH = SGU_WIDTH // SGU_GROUPS
HEAD_DIM = 64
N_HEADS = D_MODEL // HEAD_DIM
N_KV_HEADS = 8
GQA = N_HEADS // N_KV_HEADS
WINDOW = 128
WINDOW_CHUNKS = WINDOW // CHUNK
D_FF = ((8 * D_MODEL // 3 + 255) // 256) * 256
NEG = -1e30

kernel_name = "yoco_gmlp_swa_sink_stream_step"


def rmsnorm(x, g, eps=1e-6):
    xf = x.astype(jnp.float32)
    y = xf * lax.rsqrt(jnp.mean(xf * xf, axis=-1, keepdims=True) + eps)
    return (y * g.astype(jnp.float32)).astype(x.dtype)


def layernorm(x, g, b, eps=1e-5):
    xf = x.astype(jnp.float32)
    mu = jnp.mean(xf, axis=-1, keepdims=True)
    xc = xf - mu
    y = xc * lax.rsqrt(jnp.mean(xc * xc, axis=-1, keepdims=True) + eps)
    return (y * g.astype(jnp.float32) + b.astype(jnp.float32)).astype(x.dtype)


def swiglu(x, w_gate, w_up, w_down):
    return (jax.nn.silu(x @ w_gate) * (x @ w_up)) @ w_down


def chunk_causal_mask(n):
    idx = jnp.arange(n) // CHUNK
    return idx[:, None] >= idx[None, :]


def sgu_mixer(x, w_in, ln_g, ln_b, w_s, b_s, w_out, prompt):
    B, S, _ = x.shape
    u, v = jnp.split(jax.nn.gelu(x @ w_in, approximate=False), 2, axis=-1)
    v = layernorm(v, ln_g, ln_b)
    ws = jnp.where(chunk_causal_mask(SGU_CHUNK)[None], w_s, 0.0)
    if prompt:
        vb = v.reshape(B, S // SGU_CHUNK, SGU_CHUNK, SGU_GROUPS, SGU_GROUP_WIDTH)
        mixed = jnp.einsum('gij,bnjgc->bnigc', ws, vb) + b_s.T[:, :, None]
    else:
        vb = v.reshape(B, S, SGU_GROUPS, SGU_GROUP_WIDTH)
        mixed = jnp.einsum('gij,bjgc->bigc', ws[:, :S, :S], vb) + b_s[:, :S].T[:, :, None]
    mixed = mixed.reshape(B, S, SGU_WIDTH)
    return (u * mixed) @ w_out, v


def shared_kv(h, norm_kv, w_kv, k_norm):
    B, S, _ = h.shape
    kv = rmsnorm(h, norm_kv) @ w_kv
    k, v = jnp.split(kv, 2, axis=-1)
    k = rmsnorm(k.reshape(B, S, N_KV_HEADS, HEAD_DIM), k_norm)
    v = v.reshape(B, S, N_KV_HEADS, HEAD_DIM)
    return k, v


def sink_softmax(s, sinks):
    col = jnp.broadcast_to(sinks.astype(jnp.float32).reshape(N_KV_HEADS, GQA, 1, 1),
                           s.shape[:-1] + (1,))
    return jax.nn.softmax(jnp.concatenate([s, col], axis=-1), axis=-1)[..., :-1]


def project_q(hn, w_q, q_norm):
    B, S, _ = hn.shape
    q = (hn @ w_q).reshape(B, S, N_KV_HEADS, GQA, HEAD_DIM)
    return rmsnorm(q, q_norm)


def swa_prompt(hn, w_q, q_norm, sinks, w_o, k, v):
    B, S, _ = hn.shape
    nc = S // CHUNK
    nkb = (WINDOW_CHUNKS + 1) * CHUNK
    q = project_q(hn, w_q, q_norm).reshape(B, nc, CHUNK, N_KV_HEADS, GQA, HEAD_DIM)
    pad = ((0, 0), (WINDOW_CHUNKS * CHUNK, 0), (0, 0), (0, 0))
    kp = jnp.pad(k, pad).reshape(B, nc + WINDOW_CHUNKS, CHUNK, N_KV_HEADS, HEAD_DIM)
    vp = jnp.pad(v, pad).reshape(B, nc + WINDOW_CHUNKS, CHUNK, N_KV_HEADS, HEAD_DIM)
    kb = jnp.concatenate([kp[:, j:j + nc] for j in range(WINDOW_CHUNKS + 1)], axis=2)
    vb = jnp.concatenate([vp[:, j:j + nc] for j in range(WINDOW_CHUNKS + 1)], axis=2)
    s = jnp.einsum('bcqkgd,bcskd->bckgqs', q, kb,
                   preferred_element_type=jnp.float32) * (HEAD_DIM ** -0.5)
    valid = (jnp.arange(nc)[:, None] + jnp.arange(nkb)[None, :] // CHUNK - WINDOW_CHUNKS) >= 0
    s = jnp.where(valid[None, :, None, None, None, :], s, NEG)
    p = sink_softmax(s, sinks).astype(v.dtype)
    o = jnp.einsum('bckgqs,bcskd->bcqkgd', p, vb).reshape(B, S, N_HEADS * HEAD_DIM)
    return o @ w_o


def swa_sample(hn, w_q, q_norm, sinks, w_o, k_all, v_all):
    B, T, _ = hn.shape
    q = project_q(hn, w_q, q_norm)
    s = jnp.einsum('btkgd,bskd->bkgts', q, k_all,
                   preferred_element_type=jnp.float32) * (HEAD_DIM ** -0.5)
    p = sink_softmax(s, sinks).astype(v_all.dtype)
    o = jnp.einsum('bkgts,bskd->btkgd', p, v_all).reshape(B, T, N_HEADS * HEAD_DIM)
    return o @ w_o


def trunk(x, cache_k, cache_v, norm_a, w_sgu_in, sgu_ln_g, sgu_ln_b, w_sgu_s, b_sgu_s,
          w_sgu_out, norm_kv, w_kv, k_norm, norm_b, w_q, q_norm, sinks, w_o,
          norm_ffn, w_ffn_gate, w_ffn_up, w_ffn_down):
    prompt = cache_k is None
    h = x
    v_rows = []
    k = v = None
    for l in range(DEPTH):
        if l < N_A_LAYERS:
            a = l
            out, vr = sgu_mixer(rmsnorm(h, norm_a[a]), w_sgu_in[a], sgu_ln_g[a], sgu_ln_b[a],
                                w_sgu_s[a], b_sgu_s[a], w_sgu_out[a], prompt)
            h = h + out
            if not prompt:
                v_rows.append(vr)
        else:
            if l == N_A_LAYERS:
                k, v = shared_kv(h, norm_kv, w_kv, k_norm)
            b = l - N_A_LAYERS
            hn = rmsnorm(h, norm_b[b])
            if prompt:
                out = swa_prompt(hn, w_q[b], q_norm[b], sinks[b], w_o[b], k, v)
            else:
                k_all = jnp.concatenate([cache_k, k], axis=1)
                v_all = jnp.concatenate([cache_v, v], axis=1)
                out = swa_sample(hn, w_q[b], q_norm[b], sinks[b], w_o[b], k_all, v_all)
            h = h + out
        h = h + swiglu(rmsnorm(h, norm_ffn[l]), w_ffn_gate[l], w_ffn_up[l], w_ffn_down[l])
    return h, k, v, v_rows


def setup_inputs(seed: int = 0) -> dict:
    key = jax.random.key(seed)
    ks = jax.random.split(key, 32)
    f32 = jnp.float32

    def nrm(k, shape, scale):
        return jax.random.normal(k, shape, f32) * scale

    def gain(k, shape):
        return 1.0 + 0.02 * jax.random.normal(k, shape, f32)

    return {
        "x_prompt": nrm(ks[0], (BATCH, SEQ, D_MODEL), 1.0),
        "x_sample": nrm(ks[1], (DEC_BATCH, DEC_SEQ, D_MODEL), 1.0),
        "cache_k": nrm(ks[2], (DEC_BATCH, WINDOW, N_KV_HEADS, HEAD_DIM), 1.0),
        "cache_v": nrm(ks[3], (DEC_BATCH, WINDOW, N_KV_HEADS, HEAD_DIM), 1.0),
        "norm_a": gain(ks[4], (N_A_LAYERS, D_MODEL)),
        "w_sgu_in": nrm(ks[5], (N_A_LAYERS, D_MODEL, 2 * SGU_WIDTH), D_MODEL ** -0.5),
        "sgu_ln_g": gain(ks[6], (N_A_LAYERS, SGU_WIDTH)),
        "sgu_ln_b": nrm(ks[7], (N_A_LAYERS, SGU_WIDTH), 0.02),
        "w_sgu_s": nrm(ks[8], (N_A_LAYERS, SGU_GROUPS, SGU_CHUNK, SGU_CHUNK), SGU_CHUNK ** -0.5),
        "b_sgu_s": gain(ks[9], (N_A_LAYERS, SGU_GROUPS, SGU_CHUNK)),
        "w_sgu_out": nrm(ks[10], (N_A_LAYERS, SGU_WIDTH, D_MODEL), SGU_WIDTH ** -0.5),
        "norm_kv": gain(ks[11], (D_MODEL,)),
        "w_kv": nrm(ks[12], (D_MODEL, 2 * N_KV_HEADS * HEAD_DIM), D_MODEL ** -0.5),
        "k_norm": gain(ks[13], (HEAD_DIM,)),
        "norm_b": gain(ks[14], (N_B_LAYERS, D_MODEL)),
        "w_q": nrm(ks[15], (N_B_LAYERS, D_MODEL, N_HEADS * HEAD_DIM), D_MODEL ** -0.5),
        "q_norm": gain(ks[16], (N_B_LAYERS, HEAD_DIM)),
        "sinks": nrm(ks[17], (N_B_LAYERS, N_HEADS), 0.5),
        "w_o": nrm(ks[18], (N_B_LAYERS, N_HEADS * HEAD_DIM, D_MODEL), (N_HEADS * HEAD_DIM) ** -0.5),
        "norm_ffn": gain(ks[19], (DEPTH, D_MODEL)),
        "w_ffn_gate": nrm(ks[20], (DEPTH, D_MODEL, D_FF), D_MODEL ** -0.5),
        "w_ffn_up": nrm(ks[21], (DEPTH, D_MODEL, D_FF), D_MODEL ** -0.5),
        "w_ffn_down": nrm(ks[22], (DEPTH, D_FF, D_MODEL), D_FF ** -0.5),
    }


def reference(x_prompt, x_sample, cache_k, cache_v, norm_a, w_sgu_in, sgu_ln_g, sgu_ln_b,
              w_sgu_s, b_sgu_s, w_sgu_out, norm_kv, w_kv, k_norm, norm_b, w_q, q_norm,
              sinks, w_o, norm_ffn, w_ffn_gate, w_ffn_up, w_ffn_down):
    weights = (norm_a, w_sgu_in, sgu_ln_g, sgu_ln_b, w_sgu_s, b_sgu_s, w_sgu_out,
               norm_kv, w_kv, k_norm, norm_b, w_q, q_norm, sinks, w_o,
               norm_ffn, w_ffn_gate, w_ffn_up, w_ffn_down)
    y_prompt, k_p, v_p, _ = trunk(x_prompt, None, None, *weights)
    new_k_prompt = k_p[:, -WINDOW:]
    new_v_prompt = v_p[:, -WINDOW:]
    y_sample, new_k_sample, new_v_sample, v_rows = trunk(x_sample, cache_k, cache_v, *weights)
    new_sgu_v_sample = jnp.stack(v_rows, axis=0)
    return (y_prompt, y_sample, new_k_prompt, new_v_prompt, new_k_sample, new_v_sample,
            new_sgu_v_sample)
```

```python
import os
import numpy as np
from contextlib import ExitStack
import concourse.bass as bass
import concourse.mybir as mybir
from concourse.bass_utils import run_bass_kernel_spmd

F32 = mybir.dt.float32
BF16 = mybir.dt.bfloat16
AF = mybir.ActivationFunctionType
ALU = mybir.AluOpType
AX = mybir.AxisListType

D = 4096
KC = 32
DFF = 11008
JC = 86
TOK = 1184
TMAX = 416
NCORES = 8
WSLOTS = 5
WSLOT_ELEMS = 4096

PASSES = [
    (0, [128, 128, 128], ["halo", "main", "main"]),
    (384, [128, 128, 128], ["main", "main", "main"]),
    (768, [128, 128, 128, 32], ["main", "main", "main", "sample"]),
]


class Op:
    __slots__ = ("eng", "fn", "deps", "sig", "sem", "val", "is_dma", "key")


class Prog:
    ENG = ("pe", "act", "dve", "pool", "sp")

    def __init__(self):
        self.ops = {e: [] for e in self.ENG}
        self.dma_keys = {}

    def add(self, eng, fn, deps=()):
        o = Op()
        o.eng = eng
        o.fn = fn
        o.deps = [d for d in deps if d is not None]
        o.sig = False
        o.sem = None
        o.val = 0
        o.is_dma = False
        o.key = None
        self.ops[eng].append(o)
        return o

    def dma(self, eng, key, fn, deps=()):
        o = self.add(eng, fn, deps)
        o.is_dma = True
        o.key = key
        n = self.dma_keys.get(key, 0) + 1
        self.dma_keys[key] = n
        o.val = 16 * n
        return o

    def last(self, eng):
        for o in reversed(self.ops[eng]):
            if not o.is_dma and o.fn is not None:
                return o
        return None

    def barrier(self):
        lasts = [self.last(e) for e in ("pe", "act", "dve")]
        for e in ("pe", "act", "dve"):
            self.add(e, None, [l for l in lasts if l is not None])

    def emit(self, nc, es):
        for e in self.ENG:
            for o in self.ops[e]:
                for d in o.deps:
                    if d.is_dma:
                        continue
                    if d.eng == "pe" and o.eng == "pe":
                        continue
                    d.sig = True
        eng_sem = {e: es.enter_context(nc.semaphore(f"sem_{e}")) for e in self.ENG}
        key_sem = {k: es.enter_context(nc.semaphore(f"dma_{k}")) for k in self.dma_keys}
        for e in self.ENG:
            c = 0
            for o in self.ops[e]:
                if o.is_dma:
                    o.sem = key_sem[o.key]
                elif o.sig:
                    c += 1
                    o.val = c
                    o.sem = eng_sem[e]
        block = es.enter_context(nc.Block())

        def run(e, h):
            waited = {}
            for o in self.ops[e]:
                need = {}
                for d in o.deps:
                    if (not d.is_dma) and d.eng == "pe" and e == "pe":
                        continue
                    k = id(d.sem)
                    if k not in need or need[k][1] < d.val:
                        need[k] = (d.sem, d.val)
                for k, (sem, val) in need.items():
                    if waited.get(k, 0) < val:
                        h.wait_ge(sem, val)
                        waited[k] = val
                if o.fn is None:
                    continue
                inst = o.fn(h)
                if o.is_dma:
                    inst.then_inc(o.sem, 16)
                elif o.sig:
                    inst.then_inc(o.sem, 1)

        block.tensor(lambda h: run("pe", h))
        block.scalar(lambda h: run("act", h))
        block.vector(lambda h: run("dve", h))
        block.gpsimd(lambda h: run("pool", h))
        block.sync(lambda h: run("sp", h))


DECLARED = []
KSTOP = int(os.environ.get("KSTOP", "0"))
KPASS = [int(x) for x in os.environ.get("KPASS", "0,1,2").split(",")]


class StopBuild(Exception):
    pass


def build_nc():
    del DECLARED[:]
    nc = bass.Bass("TRN2", target_bir_lowering=False)
    es = ExitStack()
    P = Prog()

    def din(name, shape):
        DECLARED.append(name)
        return nc.dram_tensor(name, shape, F32, kind="ExternalInput").ap()

    def dout(name, shape):
        return nc.dram_tensor(name, shape, F32, kind="ExternalOutput").ap()

    xT = din("xT", [D, TOK])
    ckT = din("ckT", [512, 128])
    cv = din("cv", [128, 512])
    hmask_d = din("hmask", [128, 128])
    class LazyW:
        def __init__(self, name, shape):
            self.name, self.shape, self.ap = name, shape, None

        def __getitem__(self, idx):
            if self.ap is None:
                self.ap = din(self.name, self.shape)
            return self.ap[idx]

    w_in = LazyW("w_sgu_in", [D, 2 * D])
    w_out = LazyW("w_sgu_out", [D, D])
    w_kv = LazyW("w_kv", [D, 1024])
    w_q = LazyW("w_q", [D, D])
    w_o = LazyW("w_o", [D, D])
    w_gate = [LazyW(f"w_gate{l}", [D, DFF]) for l in range(2)]
    w_up = [LazyW(f"w_up{l}", [D, DFF]) for l in range(2)]
    w_down = [LazyW(f"w_down{l}", [DFF, D]) for l in range(2)]
    gcols_d = din("gcols", [128, 5 * KC])
    lncols_d = din("lncols", [128, 2 * KC])
    lnrows_d = din("lnrows", [2, D])
    wsT_d = din("wsT", [128, 8 * 128])
    mask_d = din("maskT", [128, 128])
    bs_d = din("b_s", [1, 8 * 128])
    kq_d = din("kqcol", [128, 2])
    sinks_d = din("sinks", [1, 64])

    yT = dout("yT", [D, 1056])
    kT_out = dout("kT_out", [512, 160])
    v_out = dout("v_out", [160, 512])
    sguv_out = dout("sguv_out", [32, D])

    def sb(name, shape, dt):
        return es.enter_context(nc.sbuf_tensor(name, shape, dt))

    h = sb("h", [128, KC, TMAX], F32)
    X = sb("X", [128, KC, TMAX], BF16)
    U = sb("U", [128, KC, TMAX], BF16)
    VR = sb("VR", [128, 4 * D], BF16)
    WS = [sb(f"wslot{i}", [128, WSLOT_ELEMS], BF16) for i in range(WSLOTS)]
    KT = sb("KT", [128, 4, 672], BF16)
    VT = sb("VT", [128, 6, 512], BF16)
    VO = sb("VO", [128, 4, 512], BF16)
    gcols = sb("gcols_s", [128, 5 * KC], F32)
    lncols = sb("lncols_s", [128, 2 * KC], F32)
    kqcol = sb("kqcol_s", [128, 2], F32)
    wsT = sb("wsT_s", [128, 8, 128], BF16)
    rs_bc = sb("rs_bc", [128, 8, 128], F32)
    rs_bc_s = sb("rs_bc_s", [128, 8, 32], F32)
    bs_bc = sb("bs_bc", [128, 8 * 128], F32)
    ones_bf = sb("ones_bf", [128, 128], BF16)
    bd_bf = sb("bd_bf", [128, 128], BF16)
    hmask_f = sb("hmask_f", [128, 128], F32)
    hmask_bf = sb("hmask_bf", [128, 128], BF16)
    es_t = sb("es_t", [128, 64], F32)
    eps5 = sb("eps5", [128, 1], F32)
    eps6 = sb("eps6", [128, 1], F32)
    stats = sb("stats", [128, 4, 48], F32)
    mv = sb("mv", [128, 4, 2], F32)
    rstdc = sb("rstdc", [128, 4, 2], F32)

    def vr_view(byte_off, shape, dt):
        n = int(np.prod(shape[1:]))
        if dt == F32:
            e0 = byte_off // 2
            ap = VR[:, e0:e0 + 2 * n].bitcast(F32)
        else:
            e0 = byte_off // 2
            ap = VR[:, e0:e0 + n]
        if len(shape) == 3:
            ap = ap.rearrange("p (a b) -> p a b", b=shape[2])
        elif len(shape) == 4:
            ap = ap.rearrange("p (a b c) -> p a b c", b=shape[2], c=shape[3])
        return ap

    def x_view(byte_off, shape, dt):
        Xf = X[:].rearrange("p a b -> p (a b)")
        n = int(np.prod(shape[1:]))
        e0 = byte_off // 2
        if dt == F32:
            ap = Xf[:, e0:e0 + 2 * n].bitcast(F32)
        else:
            ap = Xf[:, e0:e0 + n]
        if len(shape) == 3:
            ap = ap.rearrange("p (a b) -> p a b", b=shape[2])
        return ap

    Vtok = VR[:].rearrange("p (t f) -> p t f", f=D)
    rn_sq = vr_view(0, [128, 2, TMAX], BF16)
    rn_rstd = vr_view(2048, [128, TMAX], F32)
    rn_tmp = vr_view(4096, [128, TMAX], F32)
    actb = [vr_view(8192 + i * 6656, [128, 8, TMAX], BF16) for i in range(2)]
    sg = vr_view(8192 + 13312, [128, 2, TMAX], F32)
    sqh = vr_view(8192, [128, 2, TMAX], BF16)
    rstdh = vr_view(8192 + 1664, [128, 2, TMAX], F32)
    tmph = vr_view(8192 + 1664 + 3328, [128, 2, TMAX], F32)
    KTf = vr_view(8192 + 1664 + 6656, [128, 4, 160], F32)
    vof = vr_view(8192 + 1664 + 6656 + 2560, [128, 2, 512], F32)
    Pt = vr_view(8192, [128, 2, 2, 512], BF16)
    den = vr_view(8192 + 4096, [128, 2, 512], F32)
    rc = vr_view(8192 + 8192, [128, 2, 512], F32)
    ws_stg = vr_view(0, [128, 8, 128], F32)
    mask_stg = vr_view(4096, [128, 128], F32)
    t1 = x_view(0, [128, 2, 128], F32)
    t2 = x_view(1024, [128, 2, 128], F32)
    gb_stg = x_view(2048, [128, 2, 512], F32)
    vo_stg = x_view(2048 + 4096, [128, 512], F32)

    ps = [es.enter_context(nc.psum_tensor(f"ps{i}", [128, 512], F32)) for i in range(8)]

    class PS:
        idx = 0
        last = [[] for _ in range(8)]

    def ps_alloc():
        b = PS.idx % 8
        PS.idx += 1
        deps = PS.last[b]
        PS.last[b] = []
        return b, deps

    def ps_release(b, ops):
        PS.last[b] = list(ops)

    class WSt:
        idx = 0
        last = [None] * WSLOTS

    def wload(wdram, r0, nk, c0, ncols):
        s = WSt.idx % WSLOTS
        WSt.idx += 1
        view = WS[s][:, 0:nk * ncols].rearrange("p (k n) -> p k n", n=ncols)
        src = wdram[r0 * 128:(r0 + nk) * 128, c0:c0 + ncols].rearrange("(k p) n -> p k n", p=128)
        op = P.dma("pool", f"w{s}", lambda g, view=view, src=src: g.dma_start(out=view, in_=src),
                   deps=[WSt.last[s]])
        return s, view, op

    def wrelease(s, op):
        WSt.last[s] = op

    cst = []
    def cdma(out, in_):
        cst.append(P.dma("sp", "const", lambda q, out=out, in_=in_: q.dma_start(out=out, in_=in_)))

    cdma(gcols[:], gcols_d)
    cdma(lncols[:], lncols_d)
    cdma(kqcol[:], kq_d)
    cdma(ws_stg.rearrange("p a b -> p (a b)"), wsT_d)
    cdma(mask_stg, mask_d)
    cdma(hmask_f[:], hmask_d)
    cdma(bs_bc[:], bs_d.to_broadcast([128, 8 * 128]))
    cdma(es_t[:], sinks_d.to_broadcast([128, 64]))
    ck_stg = vr_view(8192, [128, 4, 128], F32)
    cv_stg = vr_view(8192 + 2048, [128, 512], F32)
    cdma(ck_stg, ckT.rearrange("(k p) n -> p k n", p=128))
    cdma(cv_stg, cv)

    sd = [None]

    def sdve(fn, deps=()):
        sd[0] = P.add("dve", fn, deps=list(deps) + [sd[0]])
        return sd[0]

    sdve(lambda v: v.memset(ones_bf[:], 1.0))
    sdve(lambda v: v.memset(bd_bf[:], 0.0))
    sdve(lambda v: v.memset(bd_bf[0:64, 0:64], 1.0))
    sdve(lambda v: v.memset(bd_bf[64:128, 64:128], 1.0))
    sdve(lambda v: v.memset(eps5[:], 1e-5))
    sdve(lambda v: v.memset(eps6[:], 1e-6))
    sdve(lambda v: v.memset(KT[:], 0.0))
    sdve(lambda v: v.memset(VT[:], 0.0))
    sdve(lambda v: v.memset(VO[:], 0.0))
    for g in range(8):
        sdve(lambda v, g=g: v.tensor_tensor(out=wsT[:, g, :], in0=ws_stg[:, g, :], in1=mask_stg, op=ALU.mult),
             deps=cst)
    sdve(lambda v: v.tensor_copy(out=hmask_bf[:], in_=hmask_f[:]), deps=cst)
    sdve(lambda v: v.tensor_copy(out=KT[:, :, 512:640], in_=ck_stg), deps=cst)
    last_setup_dve = sdve(lambda v: v.tensor_copy(out=VT[:, 4, :], in_=cv_stg), deps=cst)
    P.add("act", lambda a: a.activation(out=es_t[:], in_=es_t[:], func=AF.Exp), deps=cst)
    for g in range(8):
        b, bd = ps_alloc()
        m1 = P.add("pe", lambda t, b=b, g=g: t.matmul(ps[b][:, 0:128], lhsT=ones_bf[:, :], rhs=wsT[:, g, :],
                                                      start=True, stop=True), deps=bd + [last_setup_dve])
        m2 = P.add("pe", lambda t, b=b, g=g: t.matmul(ps[b][:, 128:160], lhsT=ones_bf[0:32, :],
                                                      rhs=wsT[0:32, g, 0:32], start=True, stop=True))
        e1 = P.add("dve", lambda v, b=b, g=g: v.tensor_copy(out=rs_bc[:, g, :], in_=ps[b][:, 0:128]), deps=[m2])
        e2 = P.add("dve", lambda v, b=b, g=g: v.tensor_copy(out=rs_bc_s[:, g, :], in_=ps[b][:, 128:160]), deps=[m2])
        ps_release(b, [e2])
    P.barrier()

    def rmsnorm_fm(T, nidx):
        b, bd = ps_alloc()
        mm_last = None
        mm_prev = [None, None]
        for kc in range(KC):
            sl = kc % 2
            sqo = P.add("act", lambda a, kc=kc, sl=sl: a.activation(out=rn_sq[:, sl, 0:T], in_=h[:, kc, 0:T],
                                                                     func=AF.Square), deps=[mm_prev[sl]])
            mm = P.add("pe", lambda t, kc=kc, sl=sl: t.matmul(ps[b][:, 0:T], lhsT=ones_bf[:, :], rhs=rn_sq[:, sl, 0:T],
                                                              start=(kc == 0), stop=(kc == KC - 1)),
                       deps=[sqo] + (bd if kc == 0 else []))
            mm_prev[sl] = mm
            mm_last = mm
        s1 = P.add("act", lambda a: a.activation(out=rn_tmp[:, 0:T], in_=ps[b][:, 0:T], func=AF.Sqrt,
                                                 bias=eps6[:, 0:1], scale=1.0 / D), deps=[mm_last])
        ps_release(b, [s1])
        r1 = P.add("dve", lambda v: v.reciprocal(out=rn_rstd[:, 0:T], in_=rn_tmp[:, 0:T]), deps=[s1])
        last = r1
        for kc in range(KC):
            last = P.add("dve", lambda v, kc=kc: v.scalar_tensor_tensor(
                out=X[:, kc, 0:T], in0=h[:, kc, 0:T], scalar=gcols[:, nidx * KC + kc:nidx * KC + kc + 1],
                in1=rn_rstd[:, 0:T], op0=ALU.mult, op1=ALU.mult), deps=[r1] if kc == 0 else [])
        return last

    def proj_fm(wdram, col0, ncols, T, x_ready, epilogue, src=None, nk_total=KC, row0=0):
        src = X if src is None else src
        nunits = ncols // 256
        for u in range(nunits):
            tiles = []
            k0 = 0
            while k0 < nk_total:
                nk = min(16, nk_total - k0)
                s, view, dop = wload(wdram, row0 + k0, nk, col0 + u * 256, 256)
                tiles.append((s, view, dop, k0, nk))
                k0 += nk
            mm = None
            for o2 in range(2):
                b, bd = ps_alloc()

                def emit_mm(t, b=b, o2=o2, tiles=tiles):
                    inst = None
                    for (s, view, dop, k0, nk) in tiles:
                        for k in range(nk):
                            kk = k0 + k
                            inst = t.matmul(ps[b][:, 0:T], lhsT=view[:, k, o2 * 128:(o2 + 1) * 128],
                                            rhs=src[:, kk, 0:T], start=(kk == 0), stop=(kk == nk_total - 1))
                    return inst

                mm = P.add("pe", emit_mm, deps=[t_[2] for t_ in tiles] + bd + list(x_ready))
                evs = epilogue(u * 2 + o2, b, mm)
                ps_release(b, evs)
            for (s, view, dop, k0, nk) in tiles:
                wrelease(s, mm)
        return mm

    def proj_tm(wdram, col0, ncols, tiles_tok, x_ready, epilogue):
        nunits = ncols // 256
        mm = None
        for u in range(nunits):
            tiles = []
            for k0 in (0, 16):
                s, view, dop = wload(wdram, k0, 16, col0 + u * 256, 256)
                tiles.append((s, view, dop, k0, 16))
            for ti, (c0, n) in enumerate(tiles_tok):
                b, bd = ps_alloc()

                def emit_mm(t, b=b, c0=c0, n=n, tiles=tiles):
                    inst = None
                    for (s, view, dop, k0, nk) in tiles:
                        for k in range(nk):
                            kk = k0 + k
                            inst = t.matmul(ps[b][0:n, 0:256], lhsT=X[:, kk, c0:c0 + n], rhs=view[:, k, :],
                                            start=(kk == 0), stop=(kk == KC - 1))
                    return inst

                mm = P.add("pe", emit_mm, deps=[t_[2] for t_ in tiles] + bd + list(x_ready))
                evs = epilogue(u, ti, b, mm)
                ps_release(b, evs)
            for (s, view, dop, k0, nk) in tiles:
                wrelease(s, mm)
        return mm

    def ffn(l, T, nidx):
        xr = rmsnorm_fm(T, nidx)
        P.barrier()
        nblk = (JC + 7) // 8
        prev_down_last = [None, None]
        h_last = [None] * KC
        for jb in range(nblk):
            J = min(8, JC - jb * 8)
            par = jb % 2
            act_ops = []
            for u in range(J // 2):
                col0 = jb * 1024 + u * 256
                gt = [wload(w_gate[l], k0, 16, col0, 256) + (k0,) for k0 in (0, 16)]
                ut = [wload(w_up[l], k0, 16, col0, 256) + (k0,) for k0 in (0, 16)]

                def emit(t, b, tl, o2):
                    inst = None
                    for (s_, view, dop, k0) in tl:
                        for k in range(16):
                            kk = k0 + k
                            inst = t.matmul(ps[b][:, 0:T], lhsT=view[:, k, o2 * 128:(o2 + 1) * 128],
                                            rhs=X[:, kk, 0:T], start=(kk == 0), stop=(kk == KC - 1))
                    return inst

                bgs, mmgs, bus, mmus = [], [], [], []
                for o2 in range(2):
                    bg, bgd = ps_alloc()
                    mmg = P.add("pe", lambda t, b=bg, tl=gt, o2=o2: emit(t, b, tl, o2),
                                deps=[x[2] for x in gt] + bgd + [xr])
                    bgs.append(bg)
                    mmgs.append(mmg)
                for (s_, view, dop, k0) in gt:
                    wrelease(s_, mmgs[-1])
                for o2 in range(2):
                    bu, bud = ps_alloc()
                    mmu = P.add("pe", lambda t, b=bu, tl=ut, o2=o2: emit(t, b, tl, o2),
                                deps=[x[2] for x in ut] + bud)
                    bus.append(bu)
                    mmus.append(mmu)
                for (s_, view, dop, k0) in ut:
                    wrelease(s_, mmus[-1])
                for o2 in range(2):
                    bg, bu, mmg, mmu = bgs[o2], bus[o2], mmgs[o2], mmus[o2]
                    j = u * 2 + o2
                    sl = j % 2
                    a1 = P.add("act", lambda a, b=bg, sl=sl: a.activation(out=sg[:, sl, 0:T], in_=ps[b][:, 0:T],
                                                                         func=AF.Silu),
                               deps=[mmg, ffn_sg_last[sl]])
                    ps_release(bg, [a1])
                    d1 = P.add("dve", lambda v, b=bu, sl=sl, j=j, par=par: v.tensor_tensor(
                        out=actb[par][:, j, 0:T], in0=sg[:, sl, 0:T], in1=ps[b][:, 0:T], op=ALU.mult),
                        deps=[a1, mmu, prev_down_last[par]])
                    ffn_sg_last[sl] = d1
                    ps_release(bu, [d1])
                    act_ops.append(d1)
            mm = None

            def ep(oc, b, mmop):
                a = P.add("dve", lambda v, oc=oc, b=b: v.tensor_tensor(out=h[:, oc, 0:T], in0=h[:, oc, 0:T],
                                                                      in1=ps[b][:, 0:T], op=ALU.add),
                          deps=[mmop, h_last[oc]])
                h_last[oc] = a
                return [a]

            mm = proj_fm(w_down[l], 0, D, T, [act_ops[-1]], ep, src=actb[par], nk_total=J, row0=jb * 8)
            prev_down_last[par] = mm
        P.barrier()

    ffn_sg_last = [None, None]
    att_last = [None, None]

    def headnorm_ep(T, gcol, dst_fn, extra=None):
        st = {"i": 0, "last": [None, None]}

        def ep(oc, b, mmop):
            sl = st["i"] % 2
            st["i"] += 1
            prev = st["last"][sl]
            a1 = P.add("act", lambda a, b=b, sl=sl: a.activation(out=sqh[:, sl, 0:T], in_=ps[b][:, 0:T],
                                                                 func=AF.Square), deps=[mmop, prev])
            b2, b2d = ps_alloc()
            m2 = P.add("pe", lambda t, b2=b2, sl=sl: t.matmul(ps[b2][:, 0:T], lhsT=bd_bf[:, :], rhs=sqh[:, sl, 0:T],
                                                             start=True, stop=True), deps=[a1] + b2d)
            a2 = P.add("act", lambda a, b2=b2, sl=sl: a.activation(out=tmph[:, sl, 0:T], in_=ps[b2][:, 0:T],
                                                                   func=AF.Sqrt, bias=eps6[:, 0:1], scale=1.0 / 64),
                       deps=[m2, prev])
            ps_release(b2, [a2])
            d1 = P.add("dve", lambda v, sl=sl: v.reciprocal(out=rstdh[:, sl, 0:T], in_=tmph[:, sl, 0:T]), deps=[a2, prev])
            outs = dst_fn(oc)
            d2 = P.add("dve", lambda v, b=b, sl=sl, o=outs[0]: v.scalar_tensor_tensor(
                out=o, in0=ps[b][:, 0:T], scalar=gcol, in1=rstdh[:, sl, 0:T], op0=ALU.mult, op1=ALU.mult),
                deps=[d1, mmop])
            last = d2
            if len(outs) > 1:
                last = P.add("dve", lambda v, o=outs[1], i=outs[0]: v.tensor_copy(out=o, in_=i), deps=[d2])
            st["last"][sl] = last
            return [last]

        return ep

    out_ops = []

    def ck(stage, T):
        if KSTOP == stage:
            P.barrier()
            yo = P.dma("sp", "ydbg", lambda q, T=T: q.dma_start(
                out=yT[:, 0:T].rearrange("(k p) t -> p k t", p=128), in_=h[:, :, 0:T]),
                deps=[P.last("dve"), P.last("act"), P.last("pe")])
            out_ops.append(yo)
            raise StopBuild()

    def do_pass(pi, col0, tsz, kinds, prev_out_dmas, first=False):
        T = sum(tsz)
        toffs = [sum(tsz[:i]) for i in range(len(tsz))]
        ntile = len(tsz)
        main_tiles = [(toffs[i], tsz[i]) for i in range(ntile)]
        xl = P.dma("sp", f"x{pi}", lambda q, T=T, col0=col0: q.dma_start(
            out=h[:, :, 0:T], in_=xT[:, col0:col0 + T].rearrange("(k p) t -> p k t", p=128)),
            deps=prev_out_dmas + ([P.last("dve")] if not first else []))
        P.add("act", None, [xl])
        P.add("dve", None, [xl])
        xr = rmsnorm_fm(T, 0)
        P.barrier()
        ck(1, T)

        def ep_v(u, ti, b, mmop):
            n = tsz[ti]
            a = P.add("act", lambda a, b=b, n=n, ti=ti, u=u: a.activation(
                out=Vtok[0:n, ti, u * 256:(u + 1) * 256], in_=ps[b][0:n, 0:256], func=AF.Gelu), deps=[mmop])
            return [a]

        proj_tm(w_in, D, D, main_tiles, [xr], ep_v)

        def ep_u(oc, b, mmop):
            a = P.add("act", lambda a, b=b, oc=oc: a.activation(out=U[:, oc, 0:T], in_=ps[b][:, 0:T], func=AF.Gelu),
                      deps=[mmop])
            return [a]

        x_free = proj_fm(w_in, 0, D, T, [xr], ep_u)
        P.barrier()
        ck(2, T)
        for ti in range(ntile):
            n = tsz[ti]
            bst = None
            for sgm in range(8):
                bst = P.add("dve", lambda v, ti=ti, n=n, sgm=sgm: v.bn_stats(
                    out=stats[0:n, ti, sgm * 6:(sgm + 1) * 6], in_=Vtok[0:n, ti, sgm * 512:(sgm + 1) * 512]))
            ag = P.add("dve", lambda v, ti=ti, n=n: v.bn_aggr(out=mv[0:n, ti, :], in_=stats[0:n, ti, :]), deps=[bst])
            sq_ = P.add("act", lambda a, ti=ti, n=n: a.activation(out=rstdc[0:n, ti, 0:1], in_=mv[0:n, ti, 1:2],
                                                                  func=AF.Sqrt, bias=eps5[0:n, 0:1], scale=1.0), deps=[ag])
            rcp = P.add("dve", lambda v, ti=ti, n=n: v.reciprocal(out=rstdc[0:n, ti, 1:2], in_=rstdc[0:n, ti, 0:1]),
                        deps=[sq_])
            ln = P.add("dve", lambda v, ti=ti, n=n: v.tensor_scalar(
                out=Vtok[0:n, ti, :], in0=Vtok[0:n, ti, :], scalar1=mv[0:n, ti, 0:1], scalar2=rstdc[0:n, ti, 1:2],
                op0=ALU.subtract, op1=ALU.mult), deps=[rcp])
            if kinds[ti] == "sample":
                prev_st = None
                for blk in range(8):
                    gl0 = P.dma("sp", f"lnrow{blk % 2}", lambda q, blk=blk: q.dma_start(
                        out=gb_stg[0:32, 0, :],
                        in_=lnrows_d[0:1, blk * 512:(blk + 1) * 512].to_broadcast([32, 512])),
                        deps=[prev_st, x_free])
                    gl = P.dma("sp", f"lnrow{blk % 2}", lambda q, blk=blk: q.dma_start(
                        out=gb_stg[0:32, 1, :],
                        in_=lnrows_d[1:2, blk * 512:(blk + 1) * 512].to_broadcast([32, 512])),
                        deps=[prev_st, x_free])
                    m_ = P.add("dve", lambda v, blk=blk, ti=ti: v.tensor_tensor(
                        out=vo_stg[0:32, :], in0=Vtok[0:32, ti, blk * 512:(blk + 1) * 512], in1=gb_stg[0:32, 0, :],
                        op=ALU.mult), deps=[gl0, gl, ln, prev_st])
                    a_ = P.add("dve", lambda v: v.tensor_tensor(out=vo_stg[0:32, :], in0=vo_stg[0:32, :],
                                                                in1=gb_stg[0:32, 1, :], op=ALU.add), deps=[m_])
                    prev_st = P.dma("sp", "sguv", lambda q, blk=blk: q.dma_start(
                        out=sguv_out[:, blk * 512:(blk + 1) * 512], in_=vo_stg[0:32, :]), deps=[a_])
                    out_ops.append(prev_st)
                P.add("dve", None, [prev_st])
        P.barrier()
        ck(3, T)
        tl = [None, None]
        for ti in range(ntile):
            n = tsz[ti]
            c0 = toffs[ti]
            smp = kinds[ti] == "sample"
            for c in range(KC):
                g = c // 4
                sl = c % 2
                b, bd = ps_alloc()
                mm = P.add("pe", lambda t, b=b, n=n, ti=ti, c=c, g=g: t.matmul(
                    ps[b][:, 0:n], lhsT=Vtok[0:n, ti, c * 128:(c + 1) * 128], rhs=wsT[0:n, g, 0:n],
                    start=True, stop=True), deps=bd)
                rsv = (rs_bc_s[:, g, 0:n] if smp else rs_bc[:, g, 0:n])
                o1 = P.add("dve", lambda v, sl=sl, n=n, c=c, g=g, rsv=rsv: v.scalar_tensor_tensor(
                    out=t1[:, sl, 0:n], in0=rsv, scalar=lncols[:, KC + c:KC + c + 1],
                    in1=bs_bc[:, g * 128:g * 128 + n], op0=ALU.mult, op1=ALU.add), deps=[tl[sl]])
                o2 = P.add("dve", lambda v, sl=sl, n=n, c=c, b=b: v.scalar_tensor_tensor(
                    out=t2[:, sl, 0:n], in0=ps[b][:, 0:n], scalar=lncols[:, c:c + 1], in1=t1[:, sl, 0:n],
                    op0=ALU.mult, op1=ALU.add), deps=[mm, o1])
                o3 = P.add("dve", lambda v, sl=sl, n=n, c=c, c0=c0: v.tensor_tensor(
                    out=U[:, c, c0:c0 + n], in0=t2[:, sl, 0:n], in1=U[:, c, c0:c0 + n], op=ALU.mult), deps=[o2])
                tl[sl] = o3
                ps_release(b, [o2])
        P.barrier()
        ck(4, T)

        def ep_add(oc, b, mmop):
            a = P.add("dve", lambda v, oc=oc, b=b: v.tensor_tensor(out=h[:, oc, 0:T], in0=h[:, oc, 0:T],
                                                                  in1=ps[b][:, 0:T], op=ALU.add), deps=[mmop])
            return [a]

        proj_fm(w_out, 0, D, T, [P.last("dve")], ep_add, src=U)
        P.barrier()
        ck(5, T)
        ffn(0, T, 1)
        ck(6, T)
        xr = rmsnorm_fm(T, 2)
        P.barrier()
        def kcols(ti):
            if kinds[ti] == "sample":
                return 640, 32
            return (ti + 1) * 128, tsz[ti]

        is_last = (pi == len(PASSES) - 1)

        def dst_k(oc):
            return [KTfull[:, oc, 0:T]]

        KTfull = vr_view(8192 + 1664 + 6656 + 2560 + 4096, [128, 4, TMAX], F32)
        epk = headnorm_ep(T, kqcol[:, 0:1], dst_k)
        proj_fm(w_kv, 0, 512, T, [xr], epk)
        kcp = None
        for ti in range(ntile):
            sc, n = kcols(ti)
            kcp = P.add("dve", lambda v, sc=sc, n=n, ti=ti: v.tensor_copy(
                out=KT[:, :, sc:sc + n], in_=KTfull[:, :, toffs[ti]:toffs[ti] + n]), deps=[P.last("dve")])
        if is_last:
            o_ = P.dma("sp", "kout", lambda q: q.dma_start(
                out=kT_out.rearrange("(k p) n -> p k n", p=128), in_=KTfull[:, :, 256:416]), deps=[kcp])
            out_ops.append(o_)
            kout_op = o_

        def ep_vv(u, ti, b, mmop):
            n = tsz[ti]
            smp = kinds[ti] == "sample"
            vi = 5 if smp else ti + 1
            ops_ = []
            if kinds[ti] == "halo":
                a = P.add("dve", lambda v, b=b, n=n, vi=vi, u=u: v.tensor_scalar(
                    out=VT[0:n, vi, u * 256:(u + 1) * 256], in0=ps[b][0:n, 0:256], scalar1=hmask_f[0:n, 0:1],
                    scalar2=None, op0=ALU.mult), deps=[mmop])
            else:
                a = P.add("dve", lambda v, b=b, n=n, vi=vi, u=u: v.tensor_copy(
                    out=VT[0:n, vi, u * 256:(u + 1) * 256], in_=ps[b][0:n, 0:256]), deps=[mmop])
            ops_.append(a)
            if is_last and ti >= 2:
                f_ = P.add("dve", lambda a_, b=b, n=n, ti=ti, u=u: a_.tensor_copy(
                    out=vof[0:n, ti - 2, u * 256:(u + 1) * 256], in_=ps[b][0:n, 0:256]), deps=[mmop])
                ops_.append(f_)
            return ops_

        proj_tm(w_kv, 512, 512, main_tiles, [xr], ep_vv)
        odd_tiles = [(toffs[i] + 64, 64) for i in range(ntile) if kinds[i] != "sample"]

        def ep_vo(u, ti, b, mmop):
            if kinds[ti] == "halo":
                a = P.add("dve", lambda v, b=b, ti=ti, u=u: v.tensor_scalar(
                    out=VO[0:64, ti + 1, u * 256:(u + 1) * 256], in0=ps[b][0:64, 0:256], scalar1=hmask_f[0:64, 0:1],
                    scalar2=None, op0=ALU.mult), deps=[mmop])
            else:
                a = P.add("dve", lambda v, b=b, ti=ti, u=u: v.tensor_copy(
                    out=VO[0:64, ti + 1, u * 256:(u + 1) * 256], in_=ps[b][0:64, 0:256]), deps=[mmop])
            return [a]

        proj_tm(w_kv, 512, 512, odd_tiles, [xr], ep_vo)
        if is_last:
            P.barrier()
            o_ = P.dma("sp", "vout", lambda q: q.dma_start(out=v_out[0:128, :], in_=vof[:, 0, :]),
                       deps=[P.last("act"), P.last("dve")])
            out_ops.append(o_)
            o_ = P.dma("sp", "vout", lambda q: q.dma_start(out=v_out[128:160, :], in_=vof[0:32, 1, :]),
                       deps=[P.last("act"), P.last("dve")])
            out_ops.append(o_)
            P.add("act", None, [o_, kout_op])
            P.add("dve", None, [o_, kout_op])
        P.barrier()
        ck(7, T)
        xr = rmsnorm_fm(T, 3)
        P.barrier()

        def dst_q(oc):
            return [U[:, oc, 0:T]]

        epq = headnorm_ep(T, kqcol[:, 1:2], dst_q)
        proj_fm(w_q, 0, D, T, [xr], epq)
        P.barrier()
        ck(8, T)
        it = 0
        for ti in range(ntile):
            if kinds[ti] == "halo":
                continue
            smp = kinds[ti] == "sample"
            qchunks = [(toffs[ti], 32)] if smp else [(toffs[ti], 64), (toffs[ti] + 64, 64)]
            for qi, (q0, nq) in enumerate(qchunks):
                if smp:
                    groups = [(512, 128, lambda kc: VT[0:128, 4, kc * 128:(kc + 1) * 128], ones_bf[0:128, :]),
                              (640, 32, lambda kc: VT[0:32, 5, kc * 128:(kc + 1) * 128], ones_bf[0:32, :])]
                else:
                    s_prev = ti
                    s_cur = ti + 1
                    prev_halo = (ti >= 1 and kinds[ti - 1] == "halo")
                    on_prev = hmask_bf if prev_halo else ones_bf
                    if qi == 0:
                        groups = [(s_prev * 128, 128, lambda kc, s=s_prev: VT[0:128, s, kc * 128:(kc + 1) * 128], on_prev[0:128, :]),
                                  (s_cur * 128, 64, lambda kc, s=s_cur: VT[0:64, s, kc * 128:(kc + 1) * 128], ones_bf[0:64, :])]
                    else:
                        groups = [(s_prev * 128 + 64, 64, lambda kc, s=s_prev: VO[0:64, s, kc * 128:(kc + 1) * 128], on_prev[0:64, :]),
                                  (s_cur * 128, 128, lambda kc, s=s_cur: VT[0:128, s, kc * 128:(kc + 1) * 128], ones_bf[0:128, :])]
                NQ = 8 * nq
                for k in range(8):
                    kc = k // 2
                    base = (k % 2) * 64
                    bufi = it % 2
                    it += 1
                    qv = U[base:base + 64, kc * 8:(kc + 1) * 8, q0:q0 + nq]
                    exps = []
                    for gi, (kc0, nk, vfn, onesap) in enumerate(groups):
                        b, bd = ps_alloc()
                        mm = P.add("pe", lambda t, b=b, kc=kc, base=base, kc0=kc0, nk=nk, qv=qv, NQ=NQ: t.matmul(
                            ps[b][0:nk, 0:NQ], lhsT=KT[base:base + 64, kc, kc0:kc0 + nk], rhs=qv,
                            start=True, stop=True), deps=bd)
                        ex = P.add("act", lambda a, b=b, nk=nk, bufi=bufi, gi=gi, NQ=NQ: a.activation(
                            out=Pt[0:nk, bufi, gi, 0:NQ], in_=ps[b][0:nk, 0:NQ], func=AF.Exp, scale=0.125),
                            deps=[mm, att_last[bufi]])
                        ps_release(b, [ex])
                        exps.append(ex)
                    bo, bod = ps_alloc()
                    br, brd = ps_alloc()
                    mo = None
                    for gi, (kc0, nk, vfn, onesap) in enumerate(groups):
                        mo = P.add("pe", lambda t, bo=bo, nk=nk, vfn=vfn, kc=kc, bufi=bufi, gi=gi, NQ=NQ: t.matmul(
                            ps[bo][:, 0:NQ], lhsT=vfn(kc), rhs=Pt[0:nk, bufi, gi, 0:NQ],
                            start=(gi == 0), stop=(gi == len(groups) - 1)), deps=exps + (bod if gi == 0 else []))
                    mr = None
                    for gi, (kc0, nk, vfn, onesap) in enumerate(groups):
                        mr = P.add("pe", lambda t, br=br, nk=nk, onesap=onesap, bufi=bufi, gi=gi, NQ=NQ: t.matmul(
                            ps[br][:, 0:NQ], lhsT=onesap, rhs=Pt[0:nk, bufi, gi, 0:NQ],
                            start=(gi == 0), stop=(gi == len(groups) - 1)), deps=(brd if gi == 0 else []))
                    esb = es_t[base:base + 64, k * 8:(k + 1) * 8].unsqueeze(2).to_broadcast([64, 8, nq])
                    d1 = P.add("dve", lambda v, br=br, base=base, bufi=bufi, NQ=NQ, nq=nq, esb=esb: v.tensor_tensor(
                        out=den[base:base + 64, bufi, 0:NQ].rearrange("p (g q) -> p g q", q=nq),
                        in0=ps[br][base:base + 64, 0:NQ].rearrange("p (g q) -> p g q", q=nq), in1=esb, op=ALU.add),
                        deps=[mr, att_last[bufi]])
                    ps_release(br, [d1])
                    d2 = P.add("dve", lambda v, base=base, bufi=bufi, NQ=NQ: v.reciprocal(
                        out=rc[base:base + 64, bufi, 0:NQ], in_=den[base:base + 64, bufi, 0:NQ]), deps=[d1])
                    d3 = P.add("dve", lambda v, bo=bo, base=base, bufi=bufi, NQ=NQ, nq=nq, qv=qv: v.tensor_tensor(
                        out=qv, in0=ps[bo][base:base + 64, 0:NQ].rearrange("p (g q) -> p g q", q=nq),
                        in1=rc[base:base + 64, bufi, 0:NQ].rearrange("p (g q) -> p g q", q=nq), op=ALU.mult),
                        deps=[d2, mo])
                    ps_release(bo, [d3])
                    att_last[bufi] = d3
        P.barrier()
        ck(9, T)
        if not is_last:
            P.add("dve", lambda v: v.tensor_copy(out=KT[:, :, 0:128], in_=KT[:, :, 384:512]))
            P.add("dve", lambda v: v.tensor_copy(out=VT[:, 0, :], in_=VT[:, 3, :]))
            P.add("dve", lambda v: v.tensor_copy(out=VO[0:64, 0, :], in_=VO[0:64, 3, :]))
        proj_fm(w_o, 0, D, T, [P.last("dve")], ep_add, src=U)
        P.barrier()
        ck(10, T)
        ffn(1, T, 4)
        if kinds[0] == "halo":
            hc0, n_out, y0 = 128, T - 128, 0
        else:
            hc0, n_out, y0 = 0, T, col0 - 128
        yo = P.dma("sp", f"y{pi}", lambda q, hc0=hc0, n_out=n_out, y0=y0: q.dma_start(
            out=yT[:, y0:y0 + n_out].rearrange("(k p) t -> p k t", p=128), in_=h[:, :, hc0:hc0 + n_out]),
            deps=[P.last("dve")])
        out_ops.append(yo)
        prev_out_dmas = [yo]
        return prev_out_dmas

    prev_out_dmas = []
    try:
        for n_, pi in enumerate(KPASS):
            col0, tsz, kinds = PASSES[pi]
            prev_out_dmas = do_pass(pi, col0, tsz, kinds, prev_out_dmas, first=(n_ == 0))
    except StopBuild:
        pass

    P.add("sp", None, out_ops)
    P.emit(nc, es)
    es.close()
    return nc


def _q_perm():
    perm = []
    for kc in range(4):
        for g in range(8):
            for hh in ((2 * kc) * 8 + g, (2 * kc + 1) * 8 + g):
                perm.extend(range(hh * 64, hh * 64 + 64))
    return np.asarray(perm)


def make_in_maps(inp, cores):
    f = lambda a: np.ascontiguousarray(np.asarray(a, dtype=np.float32))
    perm = _q_perm()
    shared = {
        "w_sgu_in": f(inp["w_sgu_in"][0]),
        "w_sgu_out": f(inp["w_sgu_out"][0]),
        "w_kv": f(inp["w_kv"]),
        "w_q": f(np.asarray(inp["w_q"][0])[:, perm]),
        "w_o": f(np.asarray(inp["w_o"][0])[perm, :]),
    }
    for l in range(2):
        shared[f"w_gate{l}"] = f(inp["w_ffn_gate"][l])
        shared[f"w_up{l}"] = f(inp["w_ffn_up"][l])
        shared[f"w_down{l}"] = f(inp["w_ffn_down"][l])
    col = lambda v: np.asarray(v, np.float32).reshape(KC, 128).T
    shared["gcols"] = f(np.concatenate([col(inp["norm_a"][0]), col(inp["norm_ffn"][0]), col(inp["norm_kv"]),
                                        col(inp["norm_b"][0]), col(inp["norm_ffn"][1])], axis=1))
    shared["lncols"] = f(np.concatenate([col(inp["sgu_ln_g"][0]), col(inp["sgu_ln_b"][0])], axis=1))
    shared["lnrows"] = f(np.stack([np.asarray(inp["sgu_ln_g"][0]), np.asarray(inp["sgu_ln_b"][0])], axis=0))
    ws = np.asarray(inp["w_sgu_s"][0], np.float32)
    shared["wsT"] = f(ws.transpose(2, 0, 1).reshape(128, 8 * 128))
    idx = np.arange(128) // 64
    shared["maskT"] = f((idx[None, :] >= idx[:, None]).astype(np.float32))
    shared["b_s"] = f(np.asarray(inp["b_sgu_s"][0]).reshape(1, 8 * 128))
    shared["kqcol"] = f(np.stack([np.tile(np.asarray(inp["k_norm"]), 2), np.tile(np.asarray(inp["q_norm"][0]), 2)], axis=1))
    shared["sinks"] = f(np.asarray(inp["sinks"][0]).reshape(1, 64))
    xp = np.asarray(inp["x_prompt"], np.float32)
    xs = np.asarray(inp["x_sample"], np.float32)
    ck = np.asarray(inp["cache_k"], np.float32)
    cvv = np.asarray(inp["cache_v"], np.float32)
    maps = []
    for c in cores:
        b, qi = c // 4, c % 4
        halo = xp[b, qi * 1024 - 128:qi * 1024] if qi > 0 else np.zeros((128, D), np.float32)
        toks = np.concatenate([halo, xp[b, qi * 1024:(qi + 1) * 1024], xs[c]], axis=0)
        m = dict(shared)
        m["xT"] = f(toks.T)
        m["ckT"] = f(ck[c].reshape(128, 512).T)
        m["cv"] = f(cvv[c].reshape(128, 512))
        m["hmask"] = np.full((128, 128), 1.0 if qi > 0 else 0.0, np.float32)
        maps.append(m)
    return maps


_NC_CACHE = {}


def run_cores(inp, cores, trace=False):
    if "nc" not in _NC_CACHE:
        _NC_CACHE["nc"] = build_nc()
    nc = _NC_CACHE["nc"]
    maps = make_in_maps(inp, cores)
    maps = [{k: m[k] for k in DECLARED} for m in maps]
    res = run_bass_kernel_spmd(nc, maps, core_ids=list(range(len(cores))), trace=trace)
    return res


def assemble(results, cores):
    y_prompt = np.zeros((2, 4096, D), np.float32)
    y_sample = np.zeros((8, 32, D), np.float32)
    nkp = np.zeros((2, 128, 8, 64), np.float32)
    nvp = np.zeros((2, 128, 8, 64), np.float32)
    nks = np.zeros((8, 32, 8, 64), np.float32)
    nvs = np.zeros((8, 32, 8, 64), np.float32)
    sguv = np.zeros((1, 8, 32, D), np.float32)
    for r, c in zip(results, cores):
        b, qi = c // 4, c % 4
        yT = r["yT"]
        y_prompt[b, qi * 1024:(qi + 1) * 1024] = yT[:, 0:1024].T
        y_sample[c] = yT[:, 1024:1056].T
        kT = r["kT_out"]
        vv = r["v_out"]
        if qi == 3:
            nkp[b] = kT[:, 0:128].T.reshape(128, 8, 64)
            nvp[b] = vv[0:128].reshape(128, 8, 64)
        nks[c] = kT[:, 128:160].T.reshape(32, 8, 64)
        nvs[c] = vv[128:160].reshape(32, 8, 64)
        sguv[0, c] = r["sguv_out"]
    return (y_prompt, y_sample, nkp, nvp, nks, nvs, sguv)


def kernel(**inputs):
    cores = list(range(NCORES))
    res = run_cores(inputs, cores)
    return assemble(res.results, cores)
```

```python
import os
import numpy as np
from contextlib import ExitStack
import concourse.bass as bass
import concourse.mybir as mybir
from concourse.bass_utils import run_bass_kernel_spmd

F32 = mybir.dt.float32
BF16 = mybir.dt.bfloat16
AF = mybir.ActivationFunctionType
ALU = mybir.AluOpType
AX = mybir.AxisListType

D = 4096
KC = 32
DFF = 11008
JC = 86
TOK = 1184
TMAX = 416
NCORES = 8
WSLOTS = 5
WSLOT_ELEMS = 4096

PASSES = [
    (0, [128, 128, 128], ["halo", "main", "main"]),
    (384, [128, 128, 128], ["main", "main", "main"]),
    (768, [128, 128, 128, 32], ["main", "main", "main", "sample"]),
]


class Op:
    __slots__ = ("eng", "fn", "deps", "sig", "sem", "val", "is_dma", "key")


class Prog:
    ENG = ("pe", "act", "dve", "pool", "sp")

    def __init__(self):
        self.ops = {e: [] for e in self.ENG}
        self.dma_keys = {}

    def add(self, eng, fn, deps=()):
        o = Op()
        o.eng = eng
        o.fn = fn
        o.deps = [d for d in deps if d is not None]
        o.sig = False
        o.sem = None
        o.val = 0
        o.is_dma = False
        o.key = None
        self.ops[eng].append(o)
        return o

    def dma(self, eng, key, fn, deps=()):
        o = self.add(eng, fn, deps)
        o.is_dma = True
        o.key = key
        n = self.dma_keys.get(key, 0) + 1
        self.dma_keys[key] = n
        o.val = 16 * n
        return o

    def last(self, eng):
        for o in reversed(self.ops[eng]):
            if not o.is_dma and o.fn is not None:
                return o
        return None

    def barrier(self):
        lasts = [self.last(e) for e in ("pe", "act", "dve")]
        for e in ("pe", "act", "dve"):
            self.add(e, None, [l for l in lasts if l is not None])

    def emit(self, nc, es):
        for e in self.ENG:
            for o in self.ops[e]:
                for d in o.deps:
                    if d.is_dma:
                        continue
                    if d.eng == "pe" and o.eng == "pe":
                        continue
                    d.sig = True
        eng_sem = {e: es.enter_context(nc.semaphore(f"sem_{e}")) for e in self.ENG}
        key_sem = {k: es.enter_context(nc.semaphore(f"dma_{k}")) for k in self.dma_keys}
        for e in self.ENG:
            c = 0
            for o in self.ops[e]:
                if o.is_dma:
                    o.sem = key_sem[o.key]
                elif o.sig:
                    c += 1
                    o.val = c
                    o.sem = eng_sem[e]
        block = es.enter_context(nc.Block())

        def run(e, h):
            waited = {}
            for o in self.ops[e]:
                need = {}
                for d in o.deps:
                    if (not d.is_dma) and d.eng == "pe" and e == "pe":
                        continue
                    k = id(d.sem)
                    if k not in need or need[k][1] < d.val:
                        need[k] = (d.sem, d.val)
                for k, (sem, val) in need.items():
                    if waited.get(k, 0) < val:
                        h.wait_ge(sem, val)
                        waited[k] = val
                if o.fn is None:
                    continue
                inst = o.fn(h)
                if o.is_dma:
                    inst.then_inc(o.sem, 16)
                elif o.sig:
                    inst.then_inc(o.sem, 1)

        block.tensor(lambda h: run("pe", h))
        block.scalar(lambda h: run("act", h))
        block.vector(lambda h: run("dve", h))
        block.gpsimd(lambda h: run("pool", h))
        block.sync(lambda h: run("sp", h))


DECLARED = []
KSTOP = int(os.environ.get("KSTOP", "0"))
KPASS = [int(x) for x in os.environ.get("KPASS", "0,1,2").split(",")]


class StopBuild(Exception):
    pass


def build_nc():
    del DECLARED[:]
    nc = bass.Bass("TRN2", target_bir_lowering=False)
    es = ExitStack()
    P = Prog()

    def din(name, shape):
        DECLARED.append(name)
        return nc.dram_tensor(name, shape, F32, kind="ExternalInput").ap()

    def dout(name, shape):
        return nc.dram_tensor(name, shape, F32, kind="ExternalOutput").ap()

    xT = din("xT", [D, TOK])
    ckT = din("ckT", [512, 128])
    cv = din("cv", [128, 512])
    hmask_d = din("hmask", [128, 128])
    class LazyW:
        def __init__(self, name, K, N, down=False):
            self.name, self.down, self.ap = name, down, None
            if down:
                self.shape = [(K // 128 + 7) // 8, N // 256, 128, 8 * 256]
            else:
                self.shape = [N // 256, K // 2048, 128, 16 * 256]

        def tile(self, r0, nk, c0):
            if self.ap is None:
                self.ap = din(self.name, self.shape)
            u = c0 // 256
            if self.down:
                jb = r0 // 8
                return self.ap[jb:jb + 1, u:u + 1, :, 0:nk * 256].rearrange("a b p n -> p (a b n)")
            assert nk == 16 and r0 % 16 == 0
            kh = r0 // 16
            return self.ap[u:u + 1, kh:kh + 1, :, :].rearrange("a b p n -> p (a b n)")

    w_in = LazyW("w_sgu_in", D, 2 * D)
    w_out = LazyW("w_sgu_out", D, D)
    w_kv = LazyW("w_kv", D, 1024)
    w_q = LazyW("w_q", D, D)
    w_o = LazyW("w_o", D, D)
    w_gate = [LazyW(f"w_gate{l}", D, DFF) for l in range(2)]
    w_up = [LazyW(f"w_up{l}", D, DFF) for l in range(2)]
    w_down = [LazyW(f"w_down{l}", DFF, D, down=True) for l in range(2)]
    gcols_d = din("gcols", [128, 5 * KC])
    lncols_d = din("lncols", [128, 2 * KC])
    lnrows_d = din("lnrows", [2, D])
    wsT_d = din("wsT", [128, 8 * 128])
    mask_d = din("maskT", [128, 128])
    bs_d = din("b_s", [1, 8 * 128])
    kq_d = din("kqcol", [128, 2])
    sinks_d = din("sinks", [1, 64])

    yT = dout("yT", [D, 1056])
    kT_out = dout("kT_out", [512, 160])
    v_out = dout("v_out", [160, 512])
    sguv_out = dout("sguv_out", [32, D])

    def sb(name, shape, dt):
        return es.enter_context(nc.sbuf_tensor(name, shape, dt))

    h = sb("h", [128, KC, TMAX], F32)
    X = sb("X", [128, KC, TMAX], BF16)
    U = sb("U", [128, KC, TMAX], BF16)
    VR = sb("VR", [128, 4 * D], BF16)
    WS = [sb(f"wslot{i}", [128, WSLOT_ELEMS], BF16) for i in range(WSLOTS)]
    KT = sb("KT", [128, 4, 672], BF16)
    VT = sb("VT", [128, 6, 512], BF16)
    VO = sb("VO", [128, 4, 512], BF16)
    gcols = sb("gcols_s", [128, 5 * KC], F32)
    lncols = sb("lncols_s", [128, 2 * KC], F32)
    kqcol = sb("kqcol_s", [128, 2], F32)
    wsT = sb("wsT_s", [128, 8, 128], BF16)
    rs_bc = sb("rs_bc", [128, 8, 128], F32)
    rs_bc_s = sb("rs_bc_s", [128, 8, 32], F32)
    bs_bc = sb("bs_bc", [128, 8 * 128], F32)
    ones_bf = sb("ones_bf", [128, 128], BF16)
    bd_bf = sb("bd_bf", [128, 128], BF16)
    hmask_f = sb("hmask_f", [128, 128], F32)
    hmask_bf = sb("hmask_bf", [128, 128], BF16)
    es_t = sb("es_t", [128, 64], F32)
    eps5 = sb("eps5", [128, 1], F32)
    eps6 = sb("eps6", [128, 1], F32)
    stats = sb("stats", [128, 4, 48], F32)
    mv = sb("mv", [128, 4, 2], F32)
    rstdc = sb("rstdc", [128, 4, 2], F32)

    def vr_view(byte_off, shape, dt):
        n = int(np.prod(shape[1:]))
        if dt == F32:
            e0 = byte_off // 2
            ap = VR[:, e0:e0 + 2 * n].bitcast(F32)
        else:
            e0 = byte_off // 2
            ap = VR[:, e0:e0 + n]
        if len(shape) == 3:
            ap = ap.rearrange("p (a b) -> p a b", b=shape[2])
        elif len(shape) == 4:
            ap = ap.rearrange("p (a b c) -> p a b c", b=shape[2], c=shape[3])
        return ap

    def x_view(byte_off, shape, dt):
        Xf = X[:].rearrange("p a b -> p (a b)")
        n = int(np.prod(shape[1:]))
        e0 = byte_off // 2
        if dt == F32:
            ap = Xf[:, e0:e0 + 2 * n].bitcast(F32)
        else:
            ap = Xf[:, e0:e0 + n]
        if len(shape) == 3:
            ap = ap.rearrange("p (a b) -> p a b", b=shape[2])
        return ap

    Vtok = VR[:].rearrange("p (t f) -> p t f", f=D)
    rn_sq = vr_view(0, [128, 2, TMAX], BF16)
    rn_rstd = vr_view(2048, [128, TMAX], F32)
    rn_tmp = vr_view(4096, [128, TMAX], F32)
    actb = [vr_view(8192 + i * 6656, [128, 8, TMAX], BF16) for i in range(2)]
    sg = vr_view(8192 + 13312, [128, 2, TMAX], F32)
    sqh = vr_view(8192, [128, 2, TMAX], BF16)
    rstdh = vr_view(8192 + 1664, [128, 2, TMAX], F32)
    tmph = vr_view(8192 + 1664 + 3328, [128, 2, TMAX], F32)
    KTf = vr_view(8192 + 1664 + 6656, [128, 4, 160], F32)
    vof = vr_view(8192 + 1664 + 6656 + 2560, [128, 2, 512], F32)
    Pt = vr_view(8192, [128, 2, 2, 512], BF16)
    den = vr_view(8192 + 4096, [128, 2, 512], F32)
    rc = vr_view(8192 + 8192, [128, 2, 512], F32)
    ws_stg = vr_view(0, [128, 8, 128], F32)
    mask_stg = vr_view(4096, [128, 128], F32)
    t1 = x_view(0, [128, 2, 128], F32)
    t2 = x_view(1024, [128, 2, 128], F32)
    gb_stg = x_view(2048, [128, 2, 512], F32)
    vo_stg = x_view(2048 + 4096, [128, 512], F32)

    ps = [es.enter_context(nc.psum_tensor(f"ps{i}", [128, 512], F32)) for i in range(8)]

    class PS:
        idx = 0
        last = [[] for _ in range(8)]

    def ps_alloc():
        b = PS.idx % 8
        PS.idx += 1
        deps = PS.last[b]
        PS.last[b] = []
        return b, deps

    def ps_release(b, ops):
        PS.last[b] = list(ops)

    class WSt:
        idx = 0
        last = [None] * WSLOTS

    def wload(wdram, r0, nk, c0, ncols):
        s = WSt.idx % WSLOTS
        WSt.idx += 1
        assert ncols == 256
        view = WS[s][:, 0:nk * ncols].rearrange("p (k n) -> p k n", n=ncols)
        dst = WS[s][:, 0:nk * ncols]
        src = wdram.tile(r0, nk, c0)
        op = P.dma("pool", f"w{s}", lambda g, dst=dst, src=src: g.dma_start(out=dst, in_=src),
                   deps=[WSt.last[s]])
        return s, view, op

    def wrelease(s, op):
        WSt.last[s] = op

    cst = []
    def cdma(out, in_):
        cst.append(P.dma("sp", "const", lambda q, out=out, in_=in_: q.dma_start(out=out, in_=in_)))

    cdma(gcols[:], gcols_d)
    cdma(lncols[:], lncols_d)
    cdma(kqcol[:], kq_d)
    cdma(ws_stg.rearrange("p a b -> p (a b)"), wsT_d)
    cdma(mask_stg, mask_d)
    cdma(hmask_f[:], hmask_d)
    cdma(bs_bc[:], bs_d.to_broadcast([128, 8 * 128]))
    cdma(es_t[:], sinks_d.to_broadcast([128, 64]))
    ck_stg = vr_view(8192, [128, 4, 128], F32)
    cv_stg = vr_view(8192 + 2048, [128, 512], F32)
    cdma(ck_stg, ckT.rearrange("(k p) n -> p k n", p=128))
    cdma(cv_stg, cv)

    sd = [None]

    def sdve(fn, deps=()):
        sd[0] = P.add("dve", fn, deps=list(deps) + [sd[0]])
        return sd[0]

    sdve(lambda v: v.memset(ones_bf[:], 1.0))
    sdve(lambda v: v.memset(bd_bf[:], 0.0))
    sdve(lambda v: v.memset(bd_bf[0:64, 0:64], 1.0))
    sdve(lambda v: v.memset(bd_bf[64:128, 64:128], 1.0))
    sdve(lambda v: v.memset(eps5[:], 1e-5))
    sdve(lambda v: v.memset(eps6[:], 1e-6))
    sdve(lambda v: v.memset(KT[:], 0.0))
    sdve(lambda v: v.memset(VT[:], 0.0))
    sdve(lambda v: v.memset(VO[:], 0.0))
    for g in range(8):
        sdve(lambda v, g=g: v.tensor_tensor(out=wsT[:, g, :], in0=ws_stg[:, g, :], in1=mask_stg, op=ALU.mult),
             deps=cst)
    sdve(lambda v: v.tensor_copy(out=hmask_bf[:], in_=hmask_f[:]), deps=cst)
    sdve(lambda v: v.tensor_copy(out=KT[:, :, 512:640], in_=ck_stg), deps=cst)
    last_setup_dve = sdve(lambda v: v.tensor_copy(out=VT[:, 4, :], in_=cv_stg), deps=cst)
    P.add("act", lambda a: a.activation(out=es_t[:], in_=es_t[:], func=AF.Exp), deps=cst)
    for g in range(8):
        b, bd = ps_alloc()
        m1 = P.add("pe", lambda t, b=b, g=g: t.matmul(ps[b][:, 0:128], lhsT=ones_bf[:, :], rhs=wsT[:, g, :],
                                                      start=True, stop=True), deps=bd + [last_setup_dve])
        m2 = P.add("pe", lambda t, b=b, g=g: t.matmul(ps[b][:, 128:160], lhsT=ones_bf[0:32, :],
                                                      rhs=wsT[0:32, g, 0:32], start=True, stop=True))
        e1 = P.add("dve", lambda v, b=b, g=g: v.tensor_copy(out=rs_bc[:, g, :], in_=ps[b][:, 0:128]), deps=[m2])
        e2 = P.add("dve", lambda v, b=b, g=g: v.tensor_copy(out=rs_bc_s[:, g, :], in_=ps[b][:, 128:160]), deps=[m2])
        ps_release(b, [e2])
    P.barrier()

    def rmsnorm_fm(T, nidx):
        b, bd = ps_alloc()
        mm_last = None
        mm_prev = [None, None]
        for kc in range(KC):
            sl = kc % 2
            sqo = P.add("act", lambda a, kc=kc, sl=sl: a.activation(out=rn_sq[:, sl, 0:T], in_=h[:, kc, 0:T],
                                                                     func=AF.Square), deps=[mm_prev[sl]])
            mm = P.add("pe", lambda t, kc=kc, sl=sl: t.matmul(ps[b][:, 0:T], lhsT=ones_bf[:, :], rhs=rn_sq[:, sl, 0:T],
                                                              start=(kc == 0), stop=(kc == KC - 1)),
                       deps=[sqo] + (bd if kc == 0 else []))
            mm_prev[sl] = mm
            mm_last = mm
        s1 = P.add("act", lambda a: a.activation(out=rn_tmp[:, 0:T], in_=ps[b][:, 0:T], func=AF.Sqrt,
                                                 bias=eps6[:, 0:1], scale=1.0 / D), deps=[mm_last])
        ps_release(b, [s1])
        r1 = P.add("dve", lambda v: v.reciprocal(out=rn_rstd[:, 0:T], in_=rn_tmp[:, 0:T]), deps=[s1])
        last = r1
        for kc in range(KC):
            last = P.add("dve", lambda v, kc=kc: v.scalar_tensor_tensor(
                out=X[:, kc, 0:T], in0=h[:, kc, 0:T], scalar=gcols[:, nidx * KC + kc:nidx * KC + kc + 1],
                in1=rn_rstd[:, 0:T], op0=ALU.mult, op1=ALU.mult), deps=[r1] if kc == 0 else [])
        return last

    def proj_fm(wdram, col0, ncols, T, x_ready, epilogue, src=None, nk_total=KC, row0=0):
        src = X if src is None else src
        nunits = ncols // 256
        for u in range(nunits):
            tiles = []
            k0 = 0
            while k0 < nk_total:
                nk = min(16, nk_total - k0)
                s, view, dop = wload(wdram, row0 + k0, nk, col0 + u * 256, 256)
                tiles.append((s, view, dop, k0, nk))
                k0 += nk
            mm = None
            for o2 in range(2):
                b, bd = ps_alloc()

                def emit_mm(t, b=b, o2=o2, tiles=tiles):
                    inst = None
                    for (s, view, dop, k0, nk) in tiles:
                        for k in range(nk):
                            kk = k0 + k
                            inst = t.matmul(ps[b][:, 0:T], lhsT=view[:, k, o2 * 128:(o2 + 1) * 128],
                                            rhs=src[:, kk, 0:T], start=(kk == 0), stop=(kk == nk_total - 1))
                    return inst

                mm = P.add("pe", emit_mm, deps=[t_[2] for t_ in tiles] + bd + list(x_ready))
                evs = epilogue(u * 2 + o2, b, mm)
                ps_release(b, evs)
            for (s, view, dop, k0, nk) in tiles:
                wrelease(s, mm)
        return mm

    def proj_tm(wdram, col0, ncols, tiles_tok, x_ready, epilogue):
        nunits = ncols // 256
        mm = None
        for u in range(nunits):
            tiles = []
            for k0 in (0, 16):
                s, view, dop = wload(wdram, k0, 16, col0 + u * 256, 256)
                tiles.append((s, view, dop, k0, 16))
            for ti, (c0, n) in enumerate(tiles_tok):
                b, bd = ps_alloc()

                def emit_mm(t, b=b, c0=c0, n=n, tiles=tiles):
                    inst = None
                    for (s, view, dop, k0, nk) in tiles:
                        for k in range(nk):
                            kk = k0 + k
                            inst = t.matmul(ps[b][0:n, 0:256], lhsT=X[:, kk, c0:c0 + n], rhs=view[:, k, :],
                                            start=(kk == 0), stop=(kk == KC - 1))
                    return inst

                mm = P.add("pe", emit_mm, deps=[t_[2] for t_ in tiles] + bd + list(x_ready))
                evs = epilogue(u, ti, b, mm)
                ps_release(b, evs)
            for (s, view, dop, k0, nk) in tiles:
                wrelease(s, mm)
        return mm

    def ffn(l, T, nidx):
        xr = rmsnorm_fm(T, nidx)
        P.barrier()
        nblk = (JC + 7) // 8
        prev_down_last = [None, None]
        h_last = [None] * KC
        for jb in range(nblk):
            J = min(8, JC - jb * 8)
            par = jb % 2
            act_ops = []
            for u in range(J // 2):
                col0 = jb * 1024 + u * 256
                gt = [wload(w_gate[l], k0, 16, col0, 256) + (k0,) for k0 in (0, 16)]
                ut = [wload(w_up[l], k0, 16, col0, 256) + (k0,) for k0 in (0, 16)]

                def emit(t, b, tl, o2):
                    inst = None
                    for (s_, view, dop, k0) in tl:
                        for k in range(16):
                            kk = k0 + k
                            inst = t.matmul(ps[b][:, 0:T], lhsT=view[:, k, o2 * 128:(o2 + 1) * 128],
                                            rhs=X[:, kk, 0:T], start=(kk == 0), stop=(kk == KC - 1))
                    return inst

                bgs, mmgs, bus, mmus = [], [], [], []
                for o2 in range(2):
                    bg, bgd = ps_alloc()
                    mmg = P.add("pe", lambda t, b=bg, tl=gt, o2=o2: emit(t, b, tl, o2),
                                deps=[x[2] for x in gt] + bgd + [xr])
                    bgs.append(bg)
                    mmgs.append(mmg)
                for (s_, view, dop, k0) in gt:
                    wrelease(s_, mmgs[-1])
                for o2 in range(2):
                    bu, bud = ps_alloc()
                    mmu = P.add("pe", lambda t, b=bu, tl=ut, o2=o2: emit(t, b, tl, o2),
                                deps=[x[2] for x in ut] + bud)
                    bus.append(bu)
                    mmus.append(mmu)
                for (s_, view, dop, k0) in ut:
                    wrelease(s_, mmus[-1])
                for o2 in range(2):
                    bg, bu, mmg, mmu = bgs[o2], bus[o2], mmgs[o2], mmus[o2]
                    j = u * 2 + o2
                    sl = j % 2
                    a1 = P.add("act", lambda a, b=bg, sl=sl: a.activation(out=sg[:, sl, 0:T], in_=ps[b][:, 0:T],
                                                                         func=AF.Silu),
                               deps=[mmg, ffn_sg_last[sl]])
                    ps_release(bg, [a1])
                    d1 = P.add("dve", lambda v, b=bu, sl=sl, j=j, par=par: v.tensor_tensor(
                        out=actb[par][:, j, 0:T], in0=sg[:, sl, 0:T], in1=ps[b][:, 0:T], op=ALU.mult),
                        deps=[a1, mmu, prev_down_last[par]])
                    ffn_sg_last[sl] = d1
                    ps_release(bu, [d1])
                    act_ops.append(d1)
            mm = None

            def ep(oc, b, mmop):
                a = P.add("dve", lambda v, oc=oc, b=b: v.tensor_tensor(out=h[:, oc, 0:T], in0=h[:, oc, 0:T],
                                                                      in1=ps[b][:, 0:T], op=ALU.add),
                          deps=[mmop, h_last[oc]])
                h_last[oc] = a
                return [a]

            mm = proj_fm(w_down[l], 0, D, T, [act_ops[-1]], ep, src=actb[par], nk_total=J, row0=jb * 8)
            prev_down_last[par] = mm
        P.barrier()

    ffn_sg_last = [None, None]
    att_last = [None, None]

    def headnorm_ep(T, gcol, dst_fn, extra=None):
        st = {"i": 0, "last": [None, None]}

        def ep(oc, b, mmop):
            sl = st["i"] % 2
            st["i"] += 1
            prev = st["last"][sl]
            a1 = P.add("act", lambda a, b=b, sl=sl: a.activation(out=sqh[:, sl, 0:T], in_=ps[b][:, 0:T],
                                                                 func=AF.Square), deps=[mmop, prev])
            b2, b2d = ps_alloc()
            m2 = P.add("pe", lambda t, b2=b2, sl=sl: t.matmul(ps[b2][:, 0:T], lhsT=bd_bf[:, :], rhs=sqh[:, sl, 0:T],
                                                             start=True, stop=True), deps=[a1] + b2d)
            a2 = P.add("act", lambda a, b2=b2, sl=sl: a.activation(out=tmph[:, sl, 0:T], in_=ps[b2][:, 0:T],
                                                                   func=AF.Sqrt, bias=eps6[:, 0:1], scale=1.0 / 64),
                       deps=[m2, prev])
            ps_release(b2, [a2])
            d1 = P.add("dve", lambda v, sl=sl: v.reciprocal(out=rstdh[:, sl, 0:T], in_=tmph[:, sl, 0:T]), deps=[a2, prev])
            outs = dst_fn(oc)
            d2 = P.add("dve", lambda v, b=b, sl=sl, o=outs[0]: v.scalar_tensor_tensor(
                out=o, in0=ps[b][:, 0:T], scalar=gcol, in1=rstdh[:, sl, 0:T], op0=ALU.mult, op1=ALU.mult),
                deps=[d1, mmop])
            last = d2
            if len(outs) > 1:
                last = P.add("dve", lambda v, o=outs[1], i=outs[0]: v.tensor_copy(out=o, in_=i), deps=[d2])
            st["last"][sl] = last
            return [last]

        return ep

    out_ops = []

    def ck(stage, T):
        if KSTOP == stage:
            P.barrier()
            yo = P.dma("sp", "ydbg", lambda q, T=T: q.dma_start(
                out=yT[:, 0:T].rearrange("(k p) t -> p k t", p=128), in_=h[:, :, 0:T]),
                deps=[P.last("dve"), P.last("act"), P.last("pe")])
            out_ops.append(yo)
            raise StopBuild()

    def do_pass(pi, col0, tsz, kinds, prev_out_dmas, first=False):
        T = sum(tsz)
        toffs = [sum(tsz[:i]) for i in range(len(tsz))]
        ntile = len(tsz)
        main_tiles = [(toffs[i], tsz[i]) for i in range(ntile)]
        xl = P.dma("sp", f"x{pi}", lambda q, T=T, col0=col0: q.dma_start(
            out=h[:, :, 0:T], in_=xT[:, col0:col0 + T].rearrange("(k p) t -> p k t", p=128)),
            deps=prev_out_dmas + ([P.last("dve")] if not first else []))
        P.add("act", None, [xl])
        P.add("dve", None, [xl])
        xr = rmsnorm_fm(T, 0)
        P.barrier()
        ck(1, T)

        def ep_v(u, ti, b, mmop):
            n = tsz[ti]
            a = P.add("act", lambda a, b=b, n=n, ti=ti, u=u: a.activation(
                out=Vtok[0:n, ti, u * 256:(u + 1) * 256], in_=ps[b][0:n, 0:256], func=AF.Gelu), deps=[mmop])
            return [a]

        proj_tm(w_in, D, D, main_tiles, [xr], ep_v)

        def ep_u(oc, b, mmop):
            a = P.add("act", lambda a, b=b, oc=oc: a.activation(out=U[:, oc, 0:T], in_=ps[b][:, 0:T], func=AF.Gelu),
                      deps=[mmop])
            return [a]

        x_free = proj_fm(w_in, 0, D, T, [xr], ep_u)
        P.barrier()
        ck(2, T)
        for ti in range(ntile):
            n = tsz[ti]
            bst = None
            for sgm in range(8):
                bst = P.add("dve", lambda v, ti=ti, n=n, sgm=sgm: v.bn_stats(
                    out=stats[0:n, ti, sgm * 6:(sgm + 1) * 6], in_=Vtok[0:n, ti, sgm * 512:(sgm + 1) * 512]))
            ag = P.add("dve", lambda v, ti=ti, n=n: v.bn_aggr(out=mv[0:n, ti, :], in_=stats[0:n, ti, :]), deps=[bst])
            sq_ = P.add("act", lambda a, ti=ti, n=n: a.activation(out=rstdc[0:n, ti, 0:1], in_=mv[0:n, ti, 1:2],
                                                                  func=AF.Sqrt, bias=eps5[0:n, 0:1], scale=1.0), deps=[ag])
            rcp = P.add("dve", lambda v, ti=ti, n=n: v.reciprocal(out=rstdc[0:n, ti, 1:2], in_=rstdc[0:n, ti, 0:1]),
                        deps=[sq_])
            ln = P.add("dve", lambda v, ti=ti, n=n: v.tensor_scalar(
                out=Vtok[0:n, ti, :], in0=Vtok[0:n, ti, :], scalar1=mv[0:n, ti, 0:1], scalar2=rstdc[0:n, ti, 1:2],
                op0=ALU.subtract, op1=ALU.mult), deps=[rcp])
            if kinds[ti] == "sample":
                prev_st = None
                for blk in range(8):
                    gl0 = P.dma("sp", f"lnrow{blk % 2}", lambda q, blk=blk: q.dma_start(
                        out=gb_stg[0:32, 0, :],
                        in_=lnrows_d[0:1, blk * 512:(blk + 1) * 512].to_broadcast([32, 512])),
                        deps=[prev_st, x_free])
                    gl = P.dma("sp", f"lnrow{blk % 2}", lambda q, blk=blk: q.dma_start(
                        out=gb_stg[0:32, 1, :],
                        in_=lnrows_d[1:2, blk * 512:(blk + 1) * 512].to_broadcast([32, 512])),
                        deps=[prev_st, x_free])
                    m_ = P.add("dve", lambda v, blk=blk, ti=ti: v.tensor_tensor(
                        out=vo_stg[0:32, :], in0=Vtok[0:32, ti, blk * 512:(blk + 1) * 512], in1=gb_stg[0:32, 0, :],
                        op=ALU.mult), deps=[gl0, gl, ln, prev_st])
                    a_ = P.add("dve", lambda v: v.tensor_tensor(out=vo_stg[0:32, :], in0=vo_stg[0:32, :],
                                                                in1=gb_stg[0:32, 1, :], op=ALU.add), deps=[m_])
                    prev_st = P.dma("sp", "sguv", lambda q, blk=blk: q.dma_start(
                        out=sguv_out[:, blk * 512:(blk + 1) * 512], in_=vo_stg[0:32, :]), deps=[a_])
                    out_ops.append(prev_st)
                P.add("dve", None, [prev_st])
        P.barrier()
        ck(3, T)
        tl = [None, None]
        for ti in range(ntile):
            n = tsz[ti]
            c0 = toffs[ti]
            smp = kinds[ti] == "sample"
            for c in range(KC):
                g = c // 4
                sl = c % 2
                b, bd = ps_alloc()
                mm = P.add("pe", lambda t, b=b, n=n, ti=ti, c=c, g=g: t.matmul(
                    ps[b][:, 0:n], lhsT=Vtok[0:n, ti, c * 128:(c + 1) * 128], rhs=wsT[0:n, g, 0:n],
                    start=True, stop=True), deps=bd)
                rsv = (rs_bc_s[:, g, 0:n] if smp else rs_bc[:, g, 0:n])
                o1 = P.add("dve", lambda v, sl=sl, n=n, c=c, g=g, rsv=rsv: v.scalar_tensor_tensor(
                    out=t1[:, sl, 0:n], in0=rsv, scalar=lncols[:, KC + c:KC + c + 1],
                    in1=bs_bc[:, g * 128:g * 128 + n], op0=ALU.mult, op1=ALU.add), deps=[tl[sl]])
                o2 = P.add("dve", lambda v, sl=sl, n=n, c=c, b=b: v.scalar_tensor_tensor(
                    out=t2[:, sl, 0:n], in0=ps[b][:, 0:n], scalar=lncols[:, c:c + 1], in1=t1[:, sl, 0:n],
                    op0=ALU.mult, op1=ALU.add), deps=[mm, o1])
                o3 = P.add("dve", lambda v, sl=sl, n=n, c=c, c0=c0: v.tensor_tensor(
                    out=U[:, c, c0:c0 + n], in0=t2[:, sl, 0:n], in1=U[:, c, c0:c0 + n], op=ALU.mult), deps=[o2])
                tl[sl] = o3
                ps_release(b, [o2])
        P.barrier()
        ck(4, T)

        def ep_add(oc, b, mmop):
            a = P.add("dve", lambda v, oc=oc, b=b: v.tensor_tensor(out=h[:, oc, 0:T], in0=h[:, oc, 0:T],
                                                                  in1=ps[b][:, 0:T], op=ALU.add), deps=[mmop])
            return [a]

        proj_fm(w_out, 0, D, T, [P.last("dve")], ep_add, src=U)
        P.barrier()
        ck(5, T)
        ffn(0, T, 1)
        ck(6, T)
        xr = rmsnorm_fm(T, 2)
        P.barrier()
        def kcols(ti):
            if kinds[ti] == "sample":
                return 640, 32
            return (ti + 1) * 128, tsz[ti]

        is_last = (pi == len(PASSES) - 1)

        def dst_k(oc):
            return [KTfull[:, oc, 0:T]]

        KTfull = vr_view(8192 + 1664 + 6656 + 2560 + 4096, [128, 4, TMAX], F32)
        epk = headnorm_ep(T, kqcol[:, 0:1], dst_k)
        proj_fm(w_kv, 0, 512, T, [xr], epk)
        kcp = None
        for ti in range(ntile):
            sc, n = kcols(ti)
            kcp = P.add("dve", lambda v, sc=sc, n=n, ti=ti: v.tensor_copy(
                out=KT[:, :, sc:sc + n], in_=KTfull[:, :, toffs[ti]:toffs[ti] + n]), deps=[P.last("dve")])
        if is_last:
            o_ = P.dma("sp", "kout", lambda q: q.dma_start(
                out=kT_out.rearrange("(k p) n -> p k n", p=128), in_=KTfull[:, :, 256:416]), deps=[kcp])
            out_ops.append(o_)
            kout_op = o_

        def ep_vv(u, ti, b, mmop):
            n = tsz[ti]
            smp = kinds[ti] == "sample"
            vi = 5 if smp else ti + 1
            ops_ = []
            if kinds[ti] == "halo":
                a = P.add("dve", lambda v, b=b, n=n, vi=vi, u=u: v.tensor_scalar(
                    out=VT[0:n, vi, u * 256:(u + 1) * 256], in0=ps[b][0:n, 0:256], scalar1=hmask_f[0:n, 0:1],
                    scalar2=None, op0=ALU.mult), deps=[mmop])
            else:
                a = P.add("dve", lambda v, b=b, n=n, vi=vi, u=u: v.tensor_copy(
                    out=VT[0:n, vi, u * 256:(u + 1) * 256], in_=ps[b][0:n, 0:256]), deps=[mmop])
            ops_.append(a)
            if is_last and ti >= 2:
                f_ = P.add("dve", lambda a_, b=b, n=n, ti=ti, u=u: a_.tensor_copy(
                    out=vof[0:n, ti - 2, u * 256:(u + 1) * 256], in_=ps[b][0:n, 0:256]), deps=[mmop])
                ops_.append(f_)
            return ops_

        proj_tm(w_kv, 512, 512, main_tiles, [xr], ep_vv)
        odd_tiles = [(toffs[i] + 64, 64) for i in range(ntile) if kinds[i] != "sample"]

        def ep_vo(u, ti, b, mmop):
            if kinds[ti] == "halo":
                a = P.add("dve", lambda v, b=b, ti=ti, u=u: v.tensor_scalar(
                    out=VO[0:64, ti + 1, u * 256:(u + 1) * 256], in0=ps[b][0:64, 0:256], scalar1=hmask_f[0:64, 0:1],
                    scalar2=None, op0=ALU.mult), deps=[mmop])
            else:
                a = P.add("dve", lambda v, b=b, ti=ti, u=u: v.tensor_copy(
                    out=VO[0:64, ti + 1, u * 256:(u + 1) * 256], in_=ps[b][0:64, 0:256]), deps=[mmop])
            return [a]

        proj_tm(w_kv, 512, 512, odd_tiles, [xr], ep_vo)
        if is_last:
            P.barrier()
            o_ = P.dma("sp", "vout", lambda q: q.dma_start(out=v_out[0:128, :], in_=vof[:, 0, :]),
                       deps=[P.last("act"), P.last("dve")])
            out_ops.append(o_)
            o_ = P.dma("sp", "vout", lambda q: q.dma_start(out=v_out[128:160, :], in_=vof[0:32, 1, :]),
                       deps=[P.last("act"), P.last("dve")])
            out_ops.append(o_)
            P.add("act", None, [o_, kout_op])
            P.add("dve", None, [o_, kout_op])
        P.barrier()
        ck(7, T)
        xr = rmsnorm_fm(T, 3)
        P.barrier()

        def dst_q(oc):
            return [U[:, oc, 0:T]]

        epq = headnorm_ep(T, kqcol[:, 1:2], dst_q)
        proj_fm(w_q, 0, D, T, [xr], epq)
        P.barrier()
        ck(8, T)
        it = 0
        for ti in range(ntile):
            if kinds[ti] == "halo":
                continue
            smp = kinds[ti] == "sample"
            qchunks = [(toffs[ti], 32)] if smp else [(toffs[ti], 64), (toffs[ti] + 64, 64)]
            for qi, (q0, nq) in enumerate(qchunks):
                if smp:
                    groups = [(512, 128, lambda kc: VT[0:128, 4, kc * 128:(kc + 1) * 128], ones_bf[0:128, :]),
                              (640, 32, lambda kc: VT[0:32, 5, kc * 128:(kc + 1) * 128], ones_bf[0:32, :])]
                else:
                    s_prev = ti
                    s_cur = ti + 1
                    prev_halo = (ti >= 1 and kinds[ti - 1] == "halo")
                    on_prev = hmask_bf if prev_halo else ones_bf
                    if qi == 0:
                        groups = [(s_prev * 128, 128, lambda kc, s=s_prev: VT[0:128, s, kc * 128:(kc + 1) * 128], on_prev[0:128, :]),
                                  (s_cur * 128, 64, lambda kc, s=s_cur: VT[0:64, s, kc * 128:(kc + 1) * 128], ones_bf[0:64, :])]
                    else:
                        groups = [(s_prev * 128 + 64, 64, lambda kc, s=s_prev: VO[0:64, s, kc * 128:(kc + 1) * 128], on_prev[0:64, :]),
                                  (s_cur * 128, 128, lambda kc, s=s_cur: VT[0:128, s, kc * 128:(kc + 1) * 128], ones_bf[0:128, :])]
                NQ = 8 * nq
                for k in range(8):
                    kc = k // 2
                    base = (k % 2) * 64
                    bufi = it % 2
                    it += 1
                    qv = U[base:base + 64, kc * 8:(kc + 1) * 8, q0:q0 + nq]
                    exps = []
                    for gi, (kc0, nk, vfn, onesap) in enumerate(groups):
                        b, bd = ps_alloc()
                        mm = P.add("pe", lambda t, b=b, kc=kc, base=base, kc0=kc0, nk=nk, qv=qv, NQ=NQ: t.matmul(
                            ps[b][0:nk, 0:NQ], lhsT=KT[base:base + 64, kc, kc0:kc0 + nk], rhs=qv,
                            start=True, stop=True), deps=bd)
                        ex = P.add("act", lambda a, b=b, nk=nk, bufi=bufi, gi=gi, NQ=NQ: a.activation(
                            out=Pt[0:nk, bufi, gi, 0:NQ], in_=ps[b][0:nk, 0:NQ], func=AF.Exp, scale=0.125),
                            deps=[mm, att_last[bufi]])
                        ps_release(b, [ex])
                        exps.append(ex)
                    bo, bod = ps_alloc()
                    br, brd = ps_alloc()
                    mo = None
                    for gi, (kc0, nk, vfn, onesap) in enumerate(groups):
                        mo = P.add("pe", lambda t, bo=bo, nk=nk, vfn=vfn, kc=kc, bufi=bufi, gi=gi, NQ=NQ: t.matmul(
                            ps[bo][:, 0:NQ], lhsT=vfn(kc), rhs=Pt[0:nk, bufi, gi, 0:NQ],
                            start=(gi == 0), stop=(gi == len(groups) - 1)), deps=exps + (bod if gi == 0 else []))
                    mr = None
                    for gi, (kc0, nk, vfn, onesap) in enumerate(groups):
                        mr = P.add("pe", lambda t, br=br, nk=nk, onesap=onesap, bufi=bufi, gi=gi, NQ=NQ: t.matmul(
                            ps[br][:, 0:NQ], lhsT=onesap, rhs=Pt[0:nk, bufi, gi, 0:NQ],
                            start=(gi == 0), stop=(gi == len(groups) - 1)), deps=(brd if gi == 0 else []))
                    esb = es_t[base:base + 64, k * 8:(k + 1) * 8].unsqueeze(2).to_broadcast([64, 8, nq])
                    d1 = P.add("dve", lambda v, br=br, base=base, bufi=bufi, NQ=NQ, nq=nq, esb=esb: v.tensor_tensor(
                        out=den[base:base + 64, bufi, 0:NQ].rearrange("p (g q) -> p g q", q=nq),
                        in0=ps[br][base:base + 64, 0:NQ].rearrange("p (g q) -> p g q", q=nq), in1=esb, op=ALU.add),
                        deps=[mr, att_last[bufi]])
                    ps_release(br, [d1])
                    d2 = P.add("dve", lambda v, base=base, bufi=bufi, NQ=NQ: v.reciprocal(
                        out=rc[base:base + 64, bufi, 0:NQ], in_=den[base:base + 64, bufi, 0:NQ]), deps=[d1])
                    d3 = P.add("dve", lambda v, bo=bo, base=base, bufi=bufi, NQ=NQ, nq=nq, qv=qv: v.tensor_tensor(
                        out=qv, in0=ps[bo][base:base + 64, 0:NQ].rearrange("p (g q) -> p g q", q=nq),
                        in1=rc[base:base + 64, bufi, 0:NQ].rearrange("p (g q) -> p g q", q=nq), op=ALU.mult),
                        deps=[d2, mo])
                    ps_release(bo, [d3])
                    att_last[bufi] = d3
        P.barrier()
        ck(9, T)
        if not is_last:
            P.add("dve", lambda v: v.tensor_copy(out=KT[:, :, 0:128], in_=KT[:, :, 384:512]))
            P.add("dve", lambda v: v.tensor_copy(out=VT[:, 0, :], in_=VT[:, 3, :]))
            P.add("dve", lambda v: v.tensor_copy(out=VO[0:64, 0, :], in_=VO[0:64, 3, :]))
        proj_fm(w_o, 0, D, T, [P.last("dve")], ep_add, src=U)
        P.barrier()
        ck(10, T)
        ffn(1, T, 4)
        if kinds[0] == "halo":
            hc0, n_out, y0 = 128, T - 128, 0
        else:
            hc0, n_out, y0 = 0, T, col0 - 128
        yo = P.dma("sp", f"y{pi}", lambda q, hc0=hc0, n_out=n_out, y0=y0: q.dma_start(
            out=yT[:, y0:y0 + n_out].rearrange("(k p) t -> p k t", p=128), in_=h[:, :, hc0:hc0 + n_out]),
            deps=[P.last("dve")])
        out_ops.append(yo)
        prev_out_dmas = [yo]
        return prev_out_dmas

    prev_out_dmas = []
    try:
        for n_, pi in enumerate(KPASS):
            col0, tsz, kinds = PASSES[pi]
            prev_out_dmas = do_pass(pi, col0, tsz, kinds, prev_out_dmas, first=(n_ == 0))
    except StopBuild:
        pass

    P.add("sp", None, out_ops)
    P.emit(nc, es)
    es.close()
    return nc


def _q_perm():
    perm = []
    for kc in range(4):
        for g in range(8):
            for hh in ((2 * kc) * 8 + g, (2 * kc + 1) * 8 + g):
                perm.extend(range(hh * 64, hh * 64 + 64))
    return np.asarray(perm)


def make_in_maps(inp, cores):
    f = lambda a: np.ascontiguousarray(np.asarray(a, dtype=np.float32))
    perm = _q_perm()
    def tile_std(W):
        W = np.asarray(W, np.float32)
        K, N = W.shape
        return f(W.reshape(K // 2048, 16, 128, N // 256, 256).transpose(3, 0, 2, 1, 4).reshape(N // 256, K // 2048, 128, 4096))

    def tile_down(W):
        W = np.asarray(W, np.float32)
        K, N = W.shape
        nb = (K // 128 + 7) // 8
        Wp = np.zeros((nb * 8 * 128, N), np.float32)
        Wp[:K] = W
        return f(Wp.reshape(nb, 8, 128, N // 256, 256).transpose(0, 3, 2, 1, 4).reshape(nb, N // 256, 128, 2048))

    shared = {
        "w_sgu_in": tile_std(inp["w_sgu_in"][0]),
        "w_sgu_out": tile_std(inp["w_sgu_out"][0]),
        "w_kv": tile_std(inp["w_kv"]),
        "w_q": tile_std(np.asarray(inp["w_q"][0])[:, perm]),
        "w_o": tile_std(np.asarray(inp["w_o"][0])[perm, :]),
    }
    for l in range(2):
        shared[f"w_gate{l}"] = tile_std(inp["w_ffn_gate"][l])
        shared[f"w_up{l}"] = tile_std(inp["w_ffn_up"][l])
        shared[f"w_down{l}"] = tile_down(inp["w_ffn_down"][l])
    col = lambda v: np.asarray(v, np.float32).reshape(KC, 128).T
    shared["gcols"] = f(np.concatenate([col(inp["norm_a"][0]), col(inp["norm_ffn"][0]), col(inp["norm_kv"]),
                                        col(inp["norm_b"][0]), col(inp["norm_ffn"][1])], axis=1))
    shared["lncols"] = f(np.concatenate([col(inp["sgu_ln_g"][0]), col(inp["sgu_ln_b"][0])], axis=1))
    shared["lnrows"] = f(np.stack([np.asarray(inp["sgu_ln_g"][0]), np.asarray(inp["sgu_ln_b"][0])], axis=0))
    ws = np.asarray(inp["w_sgu_s"][0], np.float32)
    shared["wsT"] = f(ws.transpose(2, 0, 1).reshape(128, 8 * 128))
    idx = np.arange(128) // 64
    shared["maskT"] = f((idx[None, :] >= idx[:, None]).astype(np.float32))
    shared["b_s"] = f(np.asarray(inp["b_sgu_s"][0]).reshape(1, 8 * 128))
    shared["kqcol"] = f(np.stack([np.tile(np.asarray(inp["k_norm"]), 2), np.tile(np.asarray(inp["q_norm"][0]), 2)], axis=1))
    shared["sinks"] = f(np.asarray(inp["sinks"][0]).reshape(1, 64))
    xp = np.asarray(inp["x_prompt"], np.float32)
    xs = np.asarray(inp["x_sample"], np.float32)
    ck = np.asarray(inp["cache_k"], np.float32)
    cvv = np.asarray(inp["cache_v"], np.float32)
    maps = []
    for c in cores:
        b, qi = c // 4, c % 4
        halo = xp[b, qi * 1024 - 128:qi * 1024] if qi > 0 else np.zeros((128, D), np.float32)
        toks = np.concatenate([halo, xp[b, qi * 1024:(qi + 1) * 1024], xs[c]], axis=0)
        m = dict(shared)
        m["xT"] = f(toks.T)
        m["ckT"] = f(ck[c].reshape(128, 512).T)
        m["cv"] = f(cvv[c].reshape(128, 512))
        m["hmask"] = np.full((128, 128), 1.0 if qi > 0 else 0.0, np.float32)
        maps.append(m)
    return maps


_NC_CACHE = {}


def run_cores(inp, cores, trace=False):
    if "nc" not in _NC_CACHE:
        _NC_CACHE["nc"] = build_nc()
    nc = _NC_CACHE["nc"]
    maps = make_in_maps(inp, cores)
    maps = [{k: m[k] for k in DECLARED} for m in maps]
    res = run_bass_kernel_spmd(nc, maps, core_ids=list(range(len(cores))), trace=trace)
    return res


def assemble(results, cores):
    y_prompt = np.zeros((2, 4096, D), np.float32)
    y_sample = np.zeros((8, 32, D), np.float32)
    nkp = np.zeros((2, 128, 8, 64), np.float32)
    nvp = np.zeros((2, 128, 8, 64), np.float32)
    nks = np.zeros((8, 32, 8, 64), np.float32)
    nvs = np.zeros((8, 32, 8, 64), np.float32)
    sguv = np.zeros((1, 8, 32, D), np.float32)
    for r, c in zip(results, cores):
        b, qi = c // 4, c % 4
        yT = r["yT"]
        y_prompt[b, qi * 1024:(qi + 1) * 1024] = yT[:, 0:1024].T
        y_sample[c] = yT[:, 1024:1056].T
        kT = r["kT_out"]
        vv = r["v_out"]
        if qi == 3:
            nkp[b] = kT[:, 0:128].T.reshape(128, 8, 64)
            nvp[b] = vv[0:128].reshape(128, 8, 64)
        nks[c] = kT[:, 128:160].T.reshape(32, 8, 64)
        nvs[c] = vv[128:160].reshape(32, 8, 64)
        sguv[0, c] = r["sguv_out"]
    return (y_prompt, y_sample, nkp, nvp, nks, nvs, sguv)


def kernel(**inputs):
    cores = list(range(NCORES))
    res = run_cores(inputs, cores)
    return assemble(res.results, cores)
```
